# Optimizing a Trainium2 kernel written in Bass

```python
import jax, jax.numpy as jnp
from jax import lax
import numpy as np

D_MODEL = 1024
BATCH = 8
SEQ = 4096
DEPTH = 4

GRID_W = 64
CTX_LEN = 256
HEAD_DIM = 64
BLOCK = 128
ROPE_BASE = 10000.0
EPS = 1e-6
NEG_INF = -1e30
CONV_DIM = 512
CONV_WIDTH = 31
WIN_HEADS = 8
WIN_KV_HEADS = 2
WINDOW = 128
RET_HEADS = 4
RET_QK_DIM = 64
RET_V_DIM = 128
RET_CHUNK = 128
GLB_HEADS = 8
GLB_KV_HEADS = 2
N_BRANCH = 4
BRANCH_DIM = 512
D_FF = 2816
FFN_CONV_WIDTH = 3

IN_SIZES = (2 * CONV_DIM,
            WIN_HEADS * HEAD_DIM, WIN_KV_HEADS * HEAD_DIM, WIN_KV_HEADS * HEAD_DIM,
            RET_HEADS * RET_QK_DIM, RET_HEADS * RET_QK_DIM, RET_HEADS * RET_V_DIM, RET_HEADS * RET_V_DIM,
            GLB_HEADS * HEAD_DIM, GLB_KV_HEADS * HEAD_DIM, GLB_KV_HEADS * HEAD_DIM,
            N_BRANCH * D_MODEL)
IN_DIM = sum(IN_SIZES)
IN_OFFSETS = tuple(int(v) for v in np.cumsum(IN_SIZES)[:-1])

kernel_name = "hybrid_gated_branch_diffusion_trunk"


def rms_norm(x, g):
    xf = x.astype(jnp.float32)
    y = xf * lax.rsqrt(jnp.mean(xf * xf, axis=-1, keepdims=True) + EPS)
    return (y * g.astype(jnp.float32)).astype(x.dtype)


def layer_norm(x, g, b):
    xf = x.astype(jnp.float32)
    mu = jnp.mean(xf, axis=-1, keepdims=True)
    var = jnp.mean(jnp.square(xf - mu), axis=-1, keepdims=True)
    y = (xf - mu) * lax.rsqrt(var + EPS) * g.astype(jnp.float32) + b.astype(jnp.float32)
    return y.astype(x.dtype)


def modulate(x, g, shift, scale):
    return rms_norm(x, g) * (1 + scale) + shift


def heads(t, n_heads):
    b, n, _ = t.shape
    return t.reshape(b, n, n_heads, -1).transpose(0, 2, 1, 3)


def merge_heads(o):
    b, hk, g, n, dh = o.shape
    return o.reshape(b, hk * g, n, dh).transpose(0, 2, 1, 3).reshape(b, n, hk * g * dh)


def axial_rope_tables(n):
    rows = n // GRID_W
    row = jnp.repeat(jnp.arange(rows), GRID_W).astype(jnp.float32)
    col = jnp.tile(jnp.arange(GRID_W), rows).astype(jnp.float32)
    half = HEAD_DIM // 2
    inv = ROPE_BASE ** (-jnp.arange(0, half, 2, dtype=jnp.float32) / half)
    ang = jnp.concatenate([row[:, None] * inv, col[:, None] * inv], axis=-1)
    return jnp.cos(ang), jnp.sin(ang)


def apply_rope(x, cos, sin):
    xf = x.astype(jnp.float32)
    x1, x2 = jnp.split(xf, 2, axis=-1)
    return jnp.concatenate([x1 * cos - x2 * sin, x2 * cos + x1 * sin], axis=-1).astype(x.dtype)


def dwconv(x, w, b):
    k = w.shape[0]
    pad = (k - 1) // 2
    y = lax.conv_general_dilated(x, w[:, None, :].astype(x.dtype), (1,), [(pad, pad)],
                                 dimension_numbers=('NWC', 'WIO', 'NWC'),
                                 feature_group_count=x.shape[-1])
    return y + b.astype(x.dtype)


def attend(q, k, v, mask, sink):
    s = jnp.einsum('bhgqd,bhkd->bhgqk', q, k).astype(jnp.float32) * (HEAD_DIM ** -0.5)
    if mask is not None:
        s = jnp.where(mask, s, NEG_INF)
    if sink is not None:
        sk = jnp.broadcast_to(sink.astype(jnp.float32)[None, :, :, None, None], s.shape[:-1] + (1,))
        s = jnp.concatenate([s, sk], axis=-1)
    p = jax.nn.softmax(s, axis=-1)
    if sink is not None:
        p = p[..., :-1]
    return jnp.einsum('bhgqk,bhkd->bhgqd', p.astype(v.dtype), v)


def conv_branch(u, dw, dw_b, ln_g, ln_b):
    a, g = jnp.split(u, 2, axis=-1)
    z = a * jax.nn.sigmoid(g)
    z = dwconv(z, dw, dw_b)
    return jax.nn.silu(layer_norm(z, ln_g, ln_b))


def window_branch(q, k, v, qc, kc, vc, sink, cos, sin, with_ctx):
    b, s, _ = q.shape
    grp = WIN_HEADS // WIN_KV_HEADS
    q = apply_rope(heads(q, WIN_HEADS), cos, sin).reshape(b, WIN_KV_HEADS, grp, s, HEAD_DIM)
    k = apply_rope(heads(k, WIN_KV_HEADS), cos, sin)
    v = heads(v, WIN_KV_HEADS)
    kc = heads(kc, WIN_KV_HEADS)
    vc = heads(vc, WIN_KV_HEADS)
    sink_g = sink.reshape(WIN_KV_HEADS, grp)
    pad = ((0, 0), (0, 0), (BLOCK, BLOCK), (0, 0))
    kp, vp = jnp.pad(k, pad), jnp.pad(v, pad)
    qi = jnp.arange(BLOCK)
    kj = jnp.arange(3 * BLOCK)
    ctx_mask = jnp.ones((BLOCK, kc.shape[2]), dtype=bool)

    def block(n):
        qb = lax.dynamic_slice_in_dim(q, n * BLOCK, BLOCK, axis=3)
        kb = lax.dynamic_slice_in_dim(kp, n * BLOCK, 3 * BLOCK, axis=2)
        vb = lax.dynamic_slice_in_dim(vp, n * BLOCK, 3 * BLOCK, axis=2)
        ipos = n * BLOCK + qi
        jpos = (n - 1) * BLOCK + kj
        band = (jnp.abs(ipos[:, None] - jpos[None, :]) <= WINDOW) & (jpos >= 0)[None, :] & (jpos < s)[None, :]
        mask = jnp.concatenate([band, ctx_mask], axis=-1)
        return attend(qb, jnp.concatenate([kb, kc], axis=2), jnp.concatenate([vb, vc], axis=2), mask, sink_g)

    o = lax.map(block, jnp.arange(s // BLOCK))
    o = o.transpose(1, 2, 3, 0, 4, 5).reshape(b, WIN_KV_HEADS, grp, s, HEAD_DIM)
    y = merge_heads(o)
    if not with_ctx:
        return y, None
    qch = heads(qc, WIN_HEADS).reshape(b, WIN_KV_HEADS, grp, qc.shape[1], HEAD_DIM)
    return y, merge_heads(attend(qch, kc, vc, None, sink_g))


def global_branch(q, k, v, qc, kc, vc, qn_g, kn_g, cos, sin, with_ctx):
    b, s, _ = q.shape
    grp = GLB_HEADS // GLB_KV_HEADS
    q = apply_rope(rms_norm(heads(q, GLB_HEADS), qn_g), cos, sin).reshape(b, GLB_KV_HEADS, grp, s, HEAD_DIM)
    k = apply_rope(rms_norm(heads(k, GLB_KV_HEADS), kn_g), cos, sin)
    v = heads(v, GLB_KV_HEADS)
    kc = rms_norm(heads(kc, GLB_KV_HEADS), kn_g)
    vc = heads(vc, GLB_KV_HEADS)
    kall = jnp.concatenate([k, kc], axis=2)
    vall = jnp.concatenate([v, vc], axis=2)

    def block(n):
        qb = lax.dynamic_slice_in_dim(q, n * BLOCK, BLOCK, axis=3)
        return attend(qb, kall, vall, None, None)

    o = lax.map(block, jnp.arange(s // BLOCK))
    o = o.transpose(1, 2, 3, 0, 4, 5).reshape(b, GLB_KV_HEADS, grp, s, HEAD_DIM)
    y = merge_heads(o)
    if not with_ctx:
        return y, None
    qch = rms_norm(heads(qc, GLB_HEADS), qn_g).reshape(b, GLB_KV_HEADS, grp, qc.shape[1], HEAD_DIM)
    return y, merge_heads(attend(qch, kc, vc, None, None))


def retention_chunks(q, k, v, gamma, s0):
    b, h, n, dk = q.shape
    dv = v.shape[-1]
    cl = RET_CHUNK
    nc = n // cl
    qc = q.reshape(b, h, nc, cl, dk)
    kc = k.reshape(b, h, nc, cl, dk)
    vc = v.reshape(b, h, nc, cl, dv)
    lg = jnp.log(gamma)
    pos = jnp.arange(cl, dtype=jnp.float32)
    diff = pos[:, None] - pos[None, :]
    decay_in = jnp.where(diff >= 0, jnp.exp(lg[:, None, None] * jnp.maximum(diff, 0.0)), 0.0)
    sc = jnp.einsum('bhnid,bhnjd->bhnij', qc, kc) * decay_in[None, :, None]
    inner = jnp.einsum('bhnij,bhnje->bhnie', sc, vc)
    k_decay = jnp.exp(lg[:, None] * (cl - 1 - pos))
    q_decay = jnp.exp(lg[:, None] * (pos + 1))
    chunk_decay = jnp.exp(lg * cl)
    u = jnp.einsum('bhncd,bhnce->bhnde', kc * k_decay[None, :, None, :, None], vc)

    def step(state, u_c):
        return chunk_decay[None, :, None, None] * state + u_c, state

    s_fin, s_prev = lax.scan(step, s0, jnp.moveaxis(u, 2, 0))
    s_prev = jnp.moveaxis(s_prev, 0, 2)
    cross = jnp.einsum('bhncd,bhnde->bhnce', qc * q_decay[None, :, None, :, None], s_prev)
    return (inner + cross).reshape(b, h, n, dv), s_fin


def retention_out(o, g, gn_g):
    mu = jnp.mean(o, axis=-1, keepdims=True)
    var = jnp.mean(jnp.square(o - mu), axis=-1, keepdims=True)
    o = (o - mu) * lax.rsqrt(var + EPS)
    b, h, n, dv = o.shape
    o = o.transpose(0, 2, 1, 3).reshape(b, n, h * dv) * gn_g.astype(jnp.float32)
    return (jax.nn.silu(g.astype(jnp.float32)) * o).astype(g.dtype)


def retention_branch(q, k, v, g, qc, kc, vc, gc, decay_logit, gn_g, with_ctx):
    scale = RET_QK_DIM ** -0.5
    prep = lambda t: heads(t, RET_HEADS).astype(jnp.float32)
    flip = lambda t: jnp.flip(t, axis=2)
    q, k, v = prep(q), prep(k) * scale, prep(v)
    qc, kc, vc = prep(qc), prep(kc) * scale, prep(vc)
    gam = jax.nn.sigmoid(decay_logit.astype(jnp.float32))
    s0 = jnp.zeros((q.shape[0], RET_HEADS, RET_QK_DIM, RET_V_DIM), jnp.float32)
    oc_f, sc_f = retention_chunks(qc, kc, vc, gam[0], s0)
    oc_b, sc_b = retention_chunks(flip(qc), flip(kc), flip(vc), gam[1], s0)
    o_f, _ = retention_chunks(q, k, v, gam[0], sc_f)
    o_b, _ = retention_chunks(flip(q), flip(k), flip(v), gam[1], sc_b)
    y = retention_out(o_f + flip(o_b), g, gn_g)
    if not with_ctx:
        return y, None
    return y, retention_out(oc_f + flip(oc_b), gc, gn_g)


def merge_branches(branches, gate_logits, w_br, w_o):
    gl = jnp.split(gate_logits, N_BRANCH, axis=-1)
    acc = jax.nn.sigmoid(gl[0]) * (branches[0] @ w_br[0])
    for i in range(1, N_BRANCH):
        acc = acc + jax.nn.sigmoid(gl[i]) * (branches[i] @ w_br[i])
    return acc @ w_o


def token_mix(h, hc, cos, sin, w_in, a_dw, a_dw_b, a_ln_g, a_ln_b, b_sink, c_decay_logit, c_gn_g,
              d_qn_g, d_kn_g, w_br, w_o, with_ctx):
    (a_u, bq, bk, bv, cq, ck, cv, cg, dq, dk, dv, gates) = jnp.split(h @ w_in, IN_OFFSETS, axis=-1)
    (a_uc, bqc, bkc, bvc, cqc, ckc, cvc, cgc, dqc, dkc, dvc, gates_c) = jnp.split(hc @ w_in, IN_OFFSETS, axis=-1)
    ya = conv_branch(a_u, a_dw, a_dw_b, a_ln_g, a_ln_b)
    yb, yb_c = window_branch(bq, bk, bv, bqc, bkc, bvc, b_sink, cos, sin, with_ctx)
    yr, yr_c = retention_branch(cq, ck, cv, cg, cqc, ckc, cvc, cgc, c_decay_logit, c_gn_g, with_ctx)
    yd, yd_c = global_branch(dq, dk, dv, dqc, dkc, dvc, d_qn_g, d_kn_g, cos, sin, with_ctx)
    y = merge_branches([ya, yb, yr, yd], gates, w_br, w_o)
    if not with_ctx:
        return y, None
    ya_c = conv_branch(a_uc, a_dw, a_dw_b, a_ln_g, a_ln_b)
    return y, merge_branches([ya_c, yb_c, yr_c, yd_c], gates_c, w_br, w_o)


def conv_ffn(h, w_up, dw, dw_b, w_down):
    u = dwconv(h @ w_up, dw, dw_b)
    a, b = jnp.split(u, 2, axis=-1)
    return (jax.nn.silu(a) * b) @ w_down


def setup_inputs(seed: int = 0) -> dict:
    key = jax.random.key(seed)
    ks = jax.random.split(key, 24)
    f32 = jnp.float32
    nrm = lambda k, shape, fan: jax.random.normal(k, shape, f32) * (fan ** -0.5)
    small = lambda k, shape: 0.02 * jax.random.normal(k, shape, f32)
    gain = lambda k, shape: 1.0 + 0.05 * jax.random.normal(k, shape, f32)
    ret_init = jnp.log(2.0 ** (5.0 + jnp.arange(RET_HEADS, dtype=f32)) - 1.0)
    return {
        "x": jax.random.normal(ks[0], (BATCH, SEQ, D_MODEL), f32),
        "c": jax.random.normal(ks[1], (BATCH, D_MODEL), f32),
        "ctx": jax.random.normal(ks[2], (BATCH, CTX_LEN, D_MODEL), f32),
        "c_ctx": jax.random.normal(ks[3], (D_MODEL,), f32),
        "ada_w": 0.5 * nrm(ks[4], (DEPTH, D_MODEL, 6 * D_MODEL), D_MODEL),
        "ada_b": small(ks[5], (DEPTH, 6 * D_MODEL)),
        "norm_g": gain(ks[6], (DEPTH, 4, D_MODEL)),
        "w_in": nrm(ks[7], (DEPTH, D_MODEL, IN_DIM), D_MODEL),
        "a_dw": nrm(ks[8], (DEPTH, CONV_WIDTH, CONV_DIM), CONV_WIDTH),
        "a_dw_b": small(ks[9], (DEPTH, CONV_DIM)),
        "a_ln_g": gain(ks[10], (DEPTH, CONV_DIM)),
        "a_ln_b": small(ks[11], (DEPTH, CONV_DIM)),
        "b_sink": jax.random.normal(ks[12], (DEPTH, WIN_HEADS), f32),
        "c_decay_logit": ret_init[None, None, :] + 0.1 * jax.random.normal(ks[13], (DEPTH, 2, RET_HEADS), f32),
        "c_gn_g": gain(ks[14], (DEPTH, RET_HEADS * RET_V_DIM)),
        "d_qn_g": gain(ks[15], (DEPTH, HEAD_DIM)),
        "d_kn_g": gain(ks[16], (DEPTH, HEAD_DIM)),
        "w_br": nrm(ks[17], (DEPTH, N_BRANCH, BRANCH_DIM, D_MODEL), BRANCH_DIM),
        "w_o": nrm(ks[18], (DEPTH, D_MODEL, D_MODEL), D_MODEL),
        "f_up": nrm(ks[19], (DEPTH, D_MODEL, 2 * D_FF), D_MODEL),
        "f_dw": nrm(ks[20], (DEPTH, FFN_CONV_WIDTH, 2 * D_FF), FFN_CONV_WIDTH),
        "f_dw_b": small(ks[21], (DEPTH, 2 * D_FF)),
        "f_down": nrm(ks[22], (DEPTH, D_FF, D_MODEL), D_FF),
    }


def reference(x, c, ctx, c_ctx, ada_w, ada_b, norm_g, w_in, a_dw, a_dw_b, a_ln_g, a_ln_b, b_sink,
              c_decay_logit, c_gn_g, d_qn_g, d_kn_g, w_br, w_o, f_up, f_dw, f_dw_b, f_down):
    cos, sin = axial_rope_tables(x.shape[1])
    sc = jax.nn.silu(c)
    scc = jax.nn.silu(c_ctx)
    xc = ctx
    for l in range(DEPTH):
        with_ctx = l < DEPTH - 1
        m = (sc @ ada_w[l] + ada_b[l])[:, None, :]
        mc = scc @ ada_w[l] + ada_b[l]
        sh1, sc1, g1, sh2, sc2, g2 = jnp.split(m, 6, axis=-1)
        sh1c, sc1c, g1c, sh2c, sc2c, g2c = jnp.split(mc, 6, axis=-1)
        h = modulate(x, norm_g[l, 0], sh1, sc1)
        hc = modulate(xc, norm_g[l, 0], sh1c, sc1c)
        y, yc = token_mix(h, hc, cos, sin, w_in[l], a_dw[l], a_dw_b[l], a_ln_g[l], a_ln_b[l], b_sink[l],
                          c_decay_logit[l], c_gn_g[l], d_qn_g[l], d_kn_g[l], w_br[l], w_o[l], with_ctx)
        x = x + g1 * rms_norm(y, norm_g[l, 1])
        h = modulate(x, norm_g[l, 2], sh2, sc2)
        x = x + g2 * rms_norm(conv_ffn(h, f_up[l], f_dw[l], f_dw_b[l], f_down[l]), norm_g[l, 3])
        if with_ctx:
            xc = xc + g1c * rms_norm(yc, norm_g[l, 1])
            hc = modulate(xc, norm_g[l, 2], sh2c, sc2c)
            xc = xc + g2c * rms_norm(conv_ffn(hc, f_up[l], f_dw[l], f_dw_b[l], f_down[l]), norm_g[l, 3])
    return x
```

```python
import numpy as np
from contextlib import ExitStack
import concourse.bass as bass
import concourse.mybir as mybir
from concourse.alu_op_type import AluOpType as ALU
from concourse.bass_utils import run_bass_kernel_spmd

AF = mybir.ActivationFunctionType
F32 = mybir.dt.float32
BF16 = mybir.dt.bfloat16
AX = mybir.AxisListType

ENGS = ['sp', 'pool', 'act', 'dve', 'pe']

T = 4352
NT = 34
D = 1024
DFF = 2816
NFC = 22
EPS = 1e-6
DEPTH = 4


class Res:
    __slots__ = ('name', 'w', 'r', 'excl')

    def __init__(self, name='', excl=False):
        self.name = name
        self.w = None
        self.r = {}
        self.excl = excl


class Prog:
    def __init__(self, nc, n_dma_sems=32):
        self.nc = nc
        self.n_dma = n_dma_sems
        self.dma_sems = [nc.alloc_semaphore(f"dq{i}") for i in range(n_dma_sems)]
        self.dma_cnt = [0] * n_dma_sems
        self.dma_rr = 0
        self.dma_rr_sw = 0
        self.epoch = 0
        self.ops = {e: [] for e in ENGS}
        self.known = {e: {} for e in ENGS}
        self._new_epoch_sems()
        self.nops = 0

    NSETS = 16

    def _new_epoch_sems(self):
        if not hasattr(self, 'sets'):
            self.sets = []
            self.setcnt = []
        si = self.epoch % self.NSETS
        if si >= len(self.sets):
            self.sets.append({e: self.nc.alloc_semaphore(f"s{si}_{e}") for e in ENGS if e != 'sp'})
            self.setcnt.append({e: 0 for e in ENGS})
        if self.epoch > 0:
            pi = (self.epoch - 1) % self.NSETS
            self.setcnt[pi] = dict(self.ecnt)
        self.esem = self.sets[si]
        self.ecnt = dict(self.setcnt[si])
        self.ecnt0 = dict(self.ecnt)
        for e in ENGS:
            self.known[e] = {k: v for k, v in self.known[e].items() if k[0] == 'd'}

    def _need(self, eng, waits, ev, same_ok):
        if ev is None:
            return
        if ev[0] == 'e':
            _, ep, e, c, sem = ev
            if ep != self.epoch:
                return
            if e == eng and same_ok and eng == 'pe':
                return
            key = ('e', e)
            val = c
        else:
            _, idx, tgt = ev
            key = ('d', idx)
            val = tgt
            sem = self.dma_sems[idx]
        if self.known[eng].get(key, 0) >= val:
            return
        if key not in waits or waits[key][1] < val:
            waits[key] = (sem, val)

    def _hazards(self, eng, reads, writes):
        waits = {}
        for r in reads:
            self._need(eng, waits, r.w, False)
        for w in writes:
            self._need(eng, waits, w.w, True)
            for ev in w.r.values():
                self._need(eng, waits, ev, True)
        for key, (sem, val) in waits.items():
            self.known[eng][key] = val
        return list(waits.values())

    def op(self, eng, fn, reads=(), writes=()):
        if any(r.excl for r in reads):
            writes = list(writes) + [r for r in reads if r.excl and r not in writes]
            reads = [r for r in reads if not r.excl]
        waits = self._hazards(eng, reads, writes)
        self.ecnt[eng] += 1
        sem = self.esem[eng]
        ev = ('e', self.epoch, eng, self.ecnt[eng], sem)
        self.ops[eng].append((waits, fn, sem))
        for r in reads:
            r.r[('e', eng)] = ev
        for w in writes:
            w.w = ev
            w.r = {}
        self.nops += 1

    def dma(self, queue, fns, reads=(), writes=()):
        if not isinstance(fns, (list, tuple)):
            fns = [fns]
        half = self.n_dma // 2
        if queue == 'pool':
            idx = half + self.dma_rr_sw
            self.dma_rr_sw = (self.dma_rr_sw + 1) % (self.n_dma - half)
        else:
            idx = self.dma_rr
            self.dma_rr = (idx + 1) % half
        waits = self._hazards(queue, reads, writes)
        prev = self.dma_cnt[idx]
        if prev > 0 and self.known[queue].get(('d', idx), 0) < prev:
            waits.append((self.dma_sems[idx], prev))
            self.known[queue][('d', idx)] = prev
        self.dma_cnt[idx] += 16 * len(fns)
        ev = ('d', idx, self.dma_cnt[idx])
        self.ops[queue].append((waits, list(fns), self.dma_sems[idx]))
        for r in reads:
            r.r[('d', idx)] = ev
        for w in writes:
            w.w = ev
            w.r = {}
        self.nops += len(fns)

    def barrier(self):
        for eng in ENGS:
            waits = []
            for e2 in ENGS:
                if e2 == 'sp' or self.ecnt[e2] == self.ecnt0[e2]:
                    continue
                if self.known[eng].get(('e', e2), 0) < self.ecnt[e2]:
                    waits.append((self.esem[e2], self.ecnt[e2]))
            for idx in range(self.n_dma):
                if self.dma_cnt[idx] > self.known[eng].get(('d', idx), 0):
                    waits.append((self.dma_sems[idx], self.dma_cnt[idx]))
                    self.known[eng][('d', idx)] = self.dma_cnt[idx]
            self.ops[eng].append((waits, None, None))
        self.epoch += 1
        self._new_epoch_sems()

    def emit(self):
        nc = self.nc
        with nc.Block() as block:
            decos = {'sp': block.sync, 'pool': block.gpsimd, 'act': block.scalar,
                     'dve': block.vector, 'pe': block.tensor}
            for name in ENGS:
                ops = self.ops[name]

                def body(engine, ops=ops):
                    for waits, fn, sem in ops:
                        for (s, val) in waits:
                            engine.wait_ge(s, val)
                        if fn is None:
                            continue
                        if isinstance(fn, list):
                            for f in fn:
                                f(engine).then_inc(sem, 16)
                        else:
                            fn(engine).then_inc(sem, 1)
                decos[name](body)
        self.ops = {e: [] for e in ENGS}


class Buf:
    __slots__ = ('t', 'r')

    def __init__(self, t, name):
        self.t = t
        self.r = Res(name)

    def __getitem__(self, k):
        return self.t[k]


class Ctx:
    pass


def build_program(nlayers=DEPTH, debug=None, phases=None):
    nc = bass.Bass("TRN2", target_bir_lowering=False)
    C = Ctx()
    C.nc = nc
    din = lambda name, shape: nc.dram_tensor(name, shape, F32, kind="ExternalInput").ap()
    xin = din("xin", [T, D])
    cc = din("cc", [128, 8, 2])
    ada_w = din("ada_w", [DEPTH, D, 6 * D])
    ada_b = din("ada_b", [DEPTH, 6 * D])
    norm_g = din("norm_g", [DEPTH, 4, D])
    w_in = din("w_in", [DEPTH, D, 8192])
    a_dw = din("a_dw", [DEPTH, 128, 4, 31])
    a_vec = din("a_vec", [DEPTH, 128, 3, 4])
    b_sink = din("b_sink", [DEPTH, 8])
    c_decay = din("c_decay", [DEPTH, 8])
    c_gn_g = din("c_gn_g", [DEPTH, 512])
    d_qk_g = din("d_qk_g", [DEPTH, 2, 64])
    w_br = din("w_br", [DEPTH, 4, 512, D])
    w_o = din("w_o", [DEPTH, D, D])
    f_up = din("f_up", [DEPTH, D, 2 * DFF])
    f_dw = din("f_dw", [DEPTH, 128, 2 * NFC, 3])
    f_dw_b = din("f_dw_b", [DEPTH, 128, 2 * NFC])
    f_down = din("f_down", [DEPTH, DFF, D])
    cst_rope = din("cst_rope", [2, 128, NT, 32])
    cst_ret = din("cst_ret", [128, 4 + 4 * 128])
    cst_mask = din("cst_mask", [128, 2 * 512])
    cst_ident = din("cst_ident", [128, 128])
    yout = nc.dram_tensor("y", [T - 256, D], F32, kind="ExternalOutput").ap()
    dbg = None
    if debug is not None:
        dbg = nc.dram_tensor("dbg", list(debug[1]), debug[2], kind="ExternalOutput").ap()
    X = nc.dram_tensor("X", [T, D], F32).ap()
    MOD = nc.dram_tensor("MOD", [DEPTH, 2, 6 * D], F32).ap()
    BR = nc.dram_tensor("BR", [4, 128, 4, T], BF16).ap()
    HTd = nc.dram_tensor("HTd", [128, 8, T], BF16).ap()
    MT = nc.dram_tensor("MT", [128, NFC, T], BF16).ap()
    rX, rMOD, rBR, rHTd, rMT = Res('X'), Res('MOD'), Res('BR'), Res('HTd'), Res('MT')

    P = Prog(nc)
    C.P = P

    with ExitStack() as top:
        uid = [0]

        def mk(es, name, shape, dt, psum=False):
            uid[0] += 1
            name = f"{name}_{uid[0]}"
            t = es.enter_context((nc.psum_tensor if psum else nc.sbuf_tensor)(name, shape, dt))
            b = Buf(t, name)
            b.r.excl = psum
            return b

        identf = mk(top, "identf", [128, 128], F32)
        identb = mk(top, "identb", [128, 128], BF16)
        cosT = mk(top, "cosT", [128, NT, 32], F32)
        sinT = mk(top, "sinT", [128, NT, 32], F32)
        retc = mk(top, "retc", [128, 4 + 4 * 128], F32)
        maskw = mk(top, "maskw", [128, 2, 512], BF16)
        onesm = mk(top, "onesm", [128, 128], F32)
        onesr = mk(top, "onesr", [128, 64], F32)

        P.dma('sp', lambda e: e.dma_start(out=identf[:], in_=cst_ident), writes=[identf.r])
        P.op('dve', lambda e: e.tensor_copy(out=identb[:], in_=identf[:]), reads=[identf.r], writes=[identb.r])
        P.dma('sp', [lambda e: e.dma_start(out=cosT[:], in_=cst_rope[0]),
                     lambda e: e.dma_start(out=sinT[:], in_=cst_rope[1])], writes=[cosT.r, sinT.r])
        P.dma('sp', lambda e: e.dma_start(out=retc[:], in_=cst_ret), writes=[retc.r])
        P.dma('pool', lambda e: e.dma_start(out=maskw[:].rearrange("p a n -> p (a n)"), in_=cst_mask), writes=[maskw.r])
        P.op('dve', lambda e: e.memset(onesm[:], 1.0 / 512), writes=[onesm.r])
        P.op('dve', lambda e: e.memset(onesr[:], 1.0), writes=[onesr.r])
        P.dma('sp', [lambda e, i=i: e.dma_start(out=X[i * 1088:(i + 1) * 1088, :], in_=xin[i * 1088:(i + 1) * 1088, :])
                     for i in range(4)], writes=[rX])

        with ExitStack() as es:
            scc = mk(es, "scc", [128, 8, 2], F32)
            wa = [mk(es, f"wa{i}", [128, 8, 512], F32) for i in range(4)]
            ab = [mk(es, f"ab{i}", [2, 512], F32) for i in range(4)]
            ao = [mk(es, f"ao{i}", [2, 512], F32) for i in range(4)]
            pa = [mk(es, f"pa{i}", [128, 512], F32, psum=True) for i in range(4)]
            P.dma('sp', lambda e: e.dma_start(out=scc[:], in_=cc), writes=[scc.r])
            P.op('act', lambda e: e.activation(out=scc[:], in_=scc[:], func=AF.Silu), reads=[scc.r], writes=[scc.r])
            it = 0
            for l in range(nlayers):
                awv = ada_w[l].rearrange("(k p) n -> p k n", p=128)
                for n in range(12):
                    s = it % 4
                    it += 1
                    P.dma('sp', lambda e, s=s, n=n, awv=awv: e.dma_start(out=wa[s][:], in_=awv[:, :, n * 512:(n + 1) * 512]),
                          writes=[wa[s].r])
                    P.dma('sp', lambda e, s=s, n=n, l=l: e.dma_start(
                        out=ab[s][:], in_=ada_b[l, n * 512:(n + 1) * 512].partition_broadcast(2)), writes=[ab[s].r])
                    for k in range(8):
                        P.op('pe', lambda e, s=s, k=k: e.matmul(pa[s][0:2, :], lhsT=scc[:, k, :], rhs=wa[s][:, k, :],
                                                               start=(k == 0), stop=(k == 7)),
                             reads=[scc.r, wa[s].r], writes=[pa[s].r])
                    P.op('dve', lambda e, s=s: e.tensor_tensor(out=ao[s][:], in0=pa[s][0:2, :], in1=ab[s][:], op=ALU.add),
                         reads=[pa[s].r, ab[s].r], writes=[ao[s].r])
                    P.dma('pool', lambda e, s=s, n=n, l=l: e.dma_start(out=MOD[l, :, n * 512:(n + 1) * 512], in_=ao[s][:]),
                          reads=[ao[s].r], writes=[rMOD])
            P.barrier()
            P.emit()

        C.__dict__.update(dict(mk=mk, identf=identf, identb=identb, cosT=cosT, sinT=sinT, retc=retc, maskw=maskw,
                               onesm=onesm, onesr=onesr, X=X, MOD=MOD, BR=BR, HTd=HTd, MT=MT,
                               rX=rX, rMOD=rMOD, rBR=rBR, rHTd=rHTd, rMT=rMT, norm_g=norm_g, w_in=w_in,
                               a_dw=a_dw, a_vec=a_vec, b_sink=b_sink, c_decay=c_decay, c_gn_g=c_gn_g,
                               d_qk_g=d_qk_g, w_br=w_br, w_o=w_o, f_up=f_up, f_dw=f_dw, f_dw_b=f_dw_b,
                               f_down=f_down))

        for l in range(nlayers):
            def want(ph):
                return phases is None or ph in phases
            with ExitStack() as es:
                hT = mk(es, "hT", [128, 8, T], BF16)
                if want('norm1'):
                    phase_norm(C, l, 0, hT, spill=True)
                if want('conv'):
                    phase_conv(C, l, hT)
                if want('win'):
                    phase_attn(C, l, hT, glb=False)
                if want('ret'):
                    phase_ret(C, l, hT)
                if want('glb'):
                    phase_attn(C, l, hT, glb=True)
            if want('merge'):
                phase_merge(C, l)
            with ExitStack() as es:
                hT = mk(es, "hT2", [128, 8, T], BF16)
                if want('norm2'):
                    phase_norm(C, l, 1, hT, spill=False)
                if want('ffn_up'):
                    phase_ffn_up(C, l, hT)
            if want('ffn_down'):
                phase_ffn_down(C, l)

        P.dma('sp', [lambda e, i=i: e.dma_start(out=yout[i * 1024:(i + 1) * 1024, :], in_=X[256 + i * 1024:256 + (i + 1) * 1024, :])
                     for i in range(4)], reads=[rX], writes=[Res('y')])
        if debug is not None:
            src = {'BR': BR, 'X': X, 'MOD': MOD, 'HTd': HTd, 'MT': MT, 'BR0': BR[0], 'BR1': BR[1], 'BR2': BR[2], 'BR3': BR[3]}[debug[0]]
            P.dma('sp', lambda e: e.dma_start(out=dbg, in_=src), reads=[rBR, rX, rMOD, rHTd, rMT], writes=[Res('dbg')])
        P.barrier()
        P.emit()
    return nc


def load_mod_tiles(C, es, l, sub, which):
    P, nc = C.P, C.nc
    out = {}
    base = 3 * sub
    tmp = [C.mk(es, f"mtmp{i}", [128, D], F32) for i in range(2)]
    for r in range(2):
        if which == 'AB':
            A = C.mk(es, f"modA{r}", [128, D], F32)
            B = C.mk(es, f"modB{r}", [128, D], F32)
            P.dma('sp', [lambda e, r=r: e.dma_start(out=tmp[0][:], in_=C.MOD[l, r, (base + 1) * D:(base + 2) * D].partition_broadcast(128)),
                         lambda e: e.dma_start(out=tmp[1][:], in_=C.norm_g[l, 2 * sub].partition_broadcast(128))],
                  reads=[C.rMOD], writes=[tmp[0].r, tmp[1].r])
            P.op('dve', lambda e, A=A: e.scalar_tensor_tensor(out=A[:], in0=tmp[0][:], scalar=1.0, in1=tmp[1][:],
                                                             op0=ALU.add, op1=ALU.mult),
                 reads=[tmp[0].r, tmp[1].r], writes=[A.r])
            P.dma('sp', lambda e, r=r, B=B: e.dma_start(out=B[:], in_=C.MOD[l, r, base * D:(base + 1) * D].partition_broadcast(128)),
                  reads=[C.rMOD], writes=[B.r])
            out[r] = (A, B)
        else:
            G = C.mk(es, f"modG{r}", [128, D], F32)
            P.dma('sp', [lambda e, r=r: e.dma_start(out=tmp[0][:], in_=C.MOD[l, r, (base + 2) * D:(base + 3) * D].partition_broadcast(128)),
                         lambda e: e.dma_start(out=tmp[1][:], in_=C.norm_g[l, 2 * sub + 1].partition_broadcast(128))],
                  reads=[C.rMOD], writes=[tmp[0].r, tmp[1].r])
            P.op('dve', lambda e, G=G: e.tensor_tensor(out=G[:], in0=tmp[0][:], in1=tmp[1][:], op=ALU.mult),
                 reads=[tmp[0].r, tmp[1].r], writes=[G.r])
            out[r] = (G,)
    return out


def phase_norm(C, l, sub, hT, spill):
    P, nc = C.P, C.nc
    with ExitStack() as es:
        mod = load_mod_tiles(C, es, l, sub, 'AB')
        xt = [C.mk(es, f"nxt{i}", [128, D], F32) for i in range(3)]
        sq = C.mk(es, "nsq", [128, D], BF16)
        ss = [C.mk(es, f"nss{i}", [128, 2], F32) for i in range(2)]
        tmp = [C.mk(es, f"ntmp{i}", [128, D], F32) for i in range(2)]
        hb = [C.mk(es, f"nhb{i}", [128, D], BF16) for i in range(2)]
        pT = [C.mk(es, f"npT{i}", [128, D], BF16, psum=True) for i in range(2)]
        def ntile(t):
            r = 1 if t < 2 else 0
            x_, s_, t_, h_, p_ = xt[t % 3], ss[t % 2], tmp[t % 2], hb[t % 2], pT[t % 2]
            P.dma('sp', lambda e, t=t, x_=x_: e.dma_start(out=x_[:], in_=C.X[t * 128:(t + 1) * 128, :]),
                  reads=[C.rX], writes=[x_.r])
            P.op('act', lambda e, x_=x_, s_=s_: e.activation(out=sq[:], in_=x_[:], func=AF.Square, scale=1.0 / 32,
                                                             accum_out=s_[:, 0:1]), reads=[x_.r], writes=[sq.r, s_.r])
            P.op('act', lambda e, s_=s_: e.activation(out=s_[:, 1:2], in_=s_[:, 0:1], func=AF.Sqrt, bias=EPS, scale=1.0),
                 reads=[s_.r], writes=[s_.r])
            P.op('dve', lambda e, s_=s_: e.reciprocal(out=s_[:, 1:2], in_=s_[:, 1:2]), reads=[s_.r], writes=[s_.r])
            A_, B_ = mod[r]
            P.op('dve', lambda e, x_=x_, s_=s_, t_=t_, A_=A_: e.scalar_tensor_tensor(
                out=t_[:], in0=x_[:], scalar=s_[:, 1:2], in1=A_[:], op0=ALU.mult, op1=ALU.mult),
                reads=[x_.r, s_.r, A_.r], writes=[t_.r])
            P.op('dve', lambda e, t_=t_, h_=h_, B_=B_: e.tensor_tensor(out=h_[:], in0=t_[:], in1=B_[:], op=ALU.add),
                 reads=[t_.r, B_.r], writes=[h_.r])
            yield
            for k in range(8):
                P.op('pe', lambda e, k=k, h_=h_, p_=p_: e.transpose(out=p_[:, k * 128:(k + 1) * 128], in_=h_[:, k * 128:(k + 1) * 128],
                                                                    identity=C.identb[:]),
                     reads=[h_.r, C.identb.r], writes=[p_.r])
            P.op('act', lambda e, t=t, p_=p_: e.activation(out=hT[:, :, t * 128:(t + 1) * 128],
                                                           in_=p_[:].rearrange("p (k n) -> p k n", k=8), func=AF.Copy),
                 reads=[p_.r], writes=[hT.r])
        gens = {}
        for t in range(NT + 1):
            if t < NT:
                gens[t] = ntile(t)
                next(gens[t])
            if t >= 1:
                next(gens.pop(t - 1), None)
        if spill:
            P.dma('sp', [lambda e, k=k: e.dma_start(out=C.HTd[:, k, :], in_=hT[:, k, :]) for k in range(8)],
                  reads=[hT.r], writes=[C.rHTd])
        P.barrier()
        P.emit()


def load_w(C, dst, src2d, c0, ncols, nk=8, d0=0):
    v = src2d.rearrange("(k p) n -> p k n", p=128)
    fns = []
    step = 2048
    for a in range(0, ncols, step):
        b = min(ncols, a + step)
        fns.append(lambda e, a=a, b=b: e.dma_start(out=dst[:, :, d0 + a:d0 + b], in_=v[:, :, c0 + a:c0 + b]))
    C.P.dma('pool', fns, writes=[dst.r])


def phase_conv(C, l, hT):
    P, nc = C.P, C.nc
    TOK = [(0, 256)] + [(256 + 512 * i, 512) for i in range(8)]
    with ExitStack() as es:
        mk = C.mk
        zb = mk(es, "czb", [128, 4, T + 60], BF16)
        wv = [mk(es, f"cw{i}", [128, 8, 256], BF16) for i in range(2)]
        dwf = mk(es, "cdwf", [128, 4, 31], F32)
        avec = mk(es, "cavec", [128, 3, 4], F32)
        diag = mk(es, "cdiag", [128, 4 * 31, 128], BF16)
        sg = [mk(es, f"csg{i}", [128, 512], F32) for i in range(2)]
        ycv2 = [mk(es, f"cycv{i}", [128, 4, 256], F32) for i in range(2)]
        ysq = mk(es, "cysq", [128, 4, 256], F32)
        mean2 = [mk(es, f"cmean{i}", [128, 256], F32) for i in range(2)]
        var2 = [mk(es, f"cvar{i}", [128, 256], F32) for i in range(2)]
        tt = [mk(es, f"ctt{i}", [128, 256], F32) for i in range(2)]
        yo = [mk(es, f"cyo{i}", [128, 4, 256], BF16) for i in range(2)]
        pp = [mk(es, f"cpp{i}", [128, 512], F32, psum=True) for i in range(8)]

        def zoff(t0):
            return t0 + 15 if t0 < 256 else t0 + 45

        P.op('dve', lambda e: e.memset(zb[:], 0.0), writes=[zb.r])
        P.dma('sp', [lambda e: e.dma_start(out=dwf[:], in_=C.a_dw[l]),
                     lambda e: e.dma_start(out=avec[:], in_=C.a_vec[l])], writes=[dwf.r, avec.r])
        for c in range(4):
            for j in range(31):
                P.op('dve', lambda e, c=c, j=j: e.tensor_scalar(out=diag[:, c * 31 + j, :], in0=C.identf[:], scalar1=dwf[:, c, j:j + 1],
                                                               scalar2=None, op0=ALU.mult),
                     reads=[C.identf.r, dwf.r], writes=[diag.r])
        it = 0
        for c in range(4):
            w_ = wv[c % 2]
            load_w(C, w_, C.w_in[l], c * 256, 256)
            for (t0, n) in TOK:
                pa_, pg_ = pp[(it % 2) * 2], pp[(it % 2) * 2 + 1]
                s_ = sg[it % 2]
                it += 1
                for half, pd in ((0, pa_), (1, pg_)):
                    for k in range(8):
                        P.op('pe', lambda e, k=k, pd=pd, half=half, w_=w_, t0=t0, n=n: e.matmul(
                            pd[:, 0:n], lhsT=w_[:, k, half * 128:(half + 1) * 128], rhs=hT[:, k, t0:t0 + n],
                            start=(k == 0), stop=(k == 7)), reads=[w_.r, hT.r], writes=[pd.r])
                P.op('act', lambda e, pg_=pg_, s_=s_, n=n: e.activation(out=s_[:, 0:n], in_=pg_[:, 0:n], func=AF.Sigmoid),
                     reads=[pg_.r], writes=[s_.r])
                P.op('dve', lambda e, pa_=pa_, s_=s_, n=n, c=c, t0=t0: e.tensor_tensor(
                    out=zb[:, c, zoff(t0):zoff(t0) + n], in0=pa_[:, 0:n], in1=s_[:, 0:n], op=ALU.mult),
                    reads=[pa_.r, s_.r], writes=[zb.r])
        def ctile(ti):
            t0, n = 256 * ti, 256
            ycv, mean, var = ycv2[ti % 2], mean2[ti % 2], var2[ti % 2]
            zo = zoff(t0)
            for c in range(4):
                pc = pp[c]
                for j in range(31):
                    P.op('pe', lambda e, c=c, j=j, pc=pc, zo=zo, n=n: e.matmul(
                        pc[:, 0:n], lhsT=diag[:, c * 31 + j, :], rhs=zb[:, c, zo + j - 15:zo + j - 15 + n],
                        start=(j == 0), stop=(j == 30)), reads=[diag.r, zb.r], writes=[pc.r])
                P.op('act', lambda e, c=c, pc=pc, n=n: e.activation(out=ycv[:, c, 0:n], in_=pc[:, 0:n], func=AF.Identity,
                                                                   bias=avec[:, 0, c:c + 1], scale=1.0),
                     reads=[pc.r, avec.r], writes=[ycv.r])
            P.op('act', lambda e, n=n: e.activation(out=ysq[:, :, 0:n], in_=ycv[:, :, 0:n], func=AF.Square),
                 reads=[ycv.r], writes=[ysq.r])
            pm, pq = pp[4], pp[5]
            for c in range(4):
                P.op('pe', lambda e, c=c, n=n: e.matmul(pm[:, 0:n], lhsT=C.onesm[:], rhs=ycv[:, c, 0:n], start=(c == 0), stop=(c == 3)),
                     reads=[C.onesm.r, ycv.r], writes=[pm.r])
            for c in range(4):
                P.op('pe', lambda e, c=c, n=n: e.matmul(pq[:, 0:n], lhsT=C.onesm[:], rhs=ysq[:, c, 0:n], start=(c == 0), stop=(c == 3)),
                     reads=[C.onesm.r, ysq.r], writes=[pq.r])
            P.op('act', lambda e, n=n: e.activation(out=mean[:, 0:n], in_=pm[:, 0:n], func=AF.Copy), reads=[pm.r], writes=[mean.r])
            P.op('dve', lambda e, n=n: e.tensor_tensor(out=var[:, 0:n], in0=mean[:, 0:n], in1=mean[:, 0:n], op=ALU.mult),
                 reads=[mean.r], writes=[var.r])
            P.op('dve', lambda e, n=n: e.tensor_tensor(out=var[:, 0:n], in0=pq[:, 0:n], in1=var[:, 0:n], op=ALU.subtract),
                 reads=[pq.r, var.r], writes=[var.r])
            P.op('act', lambda e, n=n: e.activation(out=var[:, 0:n], in_=var[:, 0:n], func=AF.Sqrt, bias=EPS, scale=1.0),
                 reads=[var.r], writes=[var.r])
            P.op('dve', lambda e, n=n: e.reciprocal(out=var[:, 0:n], in_=var[:, 0:n]), reads=[var.r], writes=[var.r])
            yield
            y_ = yo[ti % 2]
            for c in range(4):
                t_ = tt[c % 2]
                P.op('dve', lambda e, c=c, t_=t_, n=n: e.tensor_tensor(out=t_[:, 0:n], in0=ycv[:, c, 0:n], in1=mean[:, 0:n], op=ALU.subtract),
                     reads=[ycv.r, mean.r], writes=[t_.r])
                P.op('dve', lambda e, t_=t_, n=n: e.tensor_tensor(out=t_[:, 0:n], in0=t_[:, 0:n], in1=var[:, 0:n], op=ALU.mult),
                     reads=[t_.r, var.r], writes=[t_.r])
                P.op('act', lambda e, c=c, t_=t_, y_=y_, n=n: e.activation(out=y_[:, c, 0:n], in_=t_[:, 0:n], func=AF.Silu,
                                                                          scale=avec[:, 1, c:c + 1], bias=avec[:, 2, c:c + 1]),
                     reads=[t_.r, avec.r], writes=[y_.r])
            P.dma('sp', lambda e, y_=y_, t0=t0, n=n: e.dma_start(out=C.BR[0, :, :, t0:t0 + n], in_=y_[:, :, 0:n]),
                  reads=[y_.r], writes=[C.rBR])
        gens = {}
        NTC = T // 256
        for ti in range(NTC + 1):
            if ti < NTC:
                gens[ti] = ctile(ti)
                next(gens[ti])
            if ti >= 1:
                next(gens.pop(ti - 1), None)
        P.barrier()
        P.emit()


def phase_attn(C, l, hT, glb):
    P, nc = C.P, C.nc
    mk = C.mk
    col0 = 4096 + 512 - 768 + 0
    col0 = 1024 + (768 + 1536 if glb else 0)
    br = 3 if glb else 1
    with ExitStack() as es:
        wq = mk(es, "awq", [128, 8, 512], BF16)
        wkv = mk(es, "awkv", [128, 8, 256], BF16)
        QT = mk(es, "aQT", [128, 4, T], BF16)
        KT = mk(es, "aKT", [128, 2, T], BF16)
        V = mk(es, "aV", [128, NT, 2, 128], BF16)
        gq = mk(es, "agq", [128, 2, 64], F32)
        esink = mk(es, "aesink", [128, 8], F32)
        load_w(C, wq, C.w_in[l], col0, 512)
        load_w(C, wkv, C.w_in[l], col0 + 512, 256)
        P.op('dve', lambda e: e.memset(V[:], 0.0), writes=[V.r])
        P.op('dve', lambda e: e.memset(V[:, :, :, 64:65], 1.0), writes=[V.r])
        P.op('dve', lambda e: e.memset(KT[:], 0.0), writes=[KT.r])
        if glb:
            P.dma('sp', lambda e: e.dma_start(out=gq[:].rearrange("p a d -> p (a d)"),
                                              in_=C.d_qk_g[l].rearrange("a d -> (a d)").partition_broadcast(128)), writes=[gq.r])
        else:
            P.dma('sp', lambda e: e.dma_start(out=esink[64:65, :], in_=C.b_sink[l:l + 1, :]), writes=[esink.r])
            P.op('act', lambda e: e.activation(out=esink[64:65, :], in_=esink[64:65, :], func=AF.Exp), reads=[esink.r], writes=[esink.r])
        with ExitStack() as es1:
            pq = [mk(es1, f"apq{i}", [128, 512], F32, psum=True) for i in range(2)]
            pk = [mk(es1, f"apk{i}", [128, 512], F32, psum=True) for i in range(2)]
            pT = [mk(es1, f"apT{i}", [128, 1024], BF16, psum=True) for i in range(2)]
            sq = mk(es1, "asq", [128, 640], F32)
            ssum = [mk(es1, f"assum{i}", [128, 10], F32) for i in range(2)]
            qn = [mk(es1, f"aqn{i}", [128, 640], F32) for i in range(2)]
            r1 = [mk(es1, f"ar1{i}", [128, 10, 32], F32) for i in range(2)]
            r2 = [mk(es1, f"ar2{i}", [128, 10, 32], F32) for i in range(2)]
            qb = [mk(es1, f"aqb{i}", [128, 640], BF16) for i in range(2)]
            def tile1(t):
                i2 = t % 2
                pq_, pk_, pT_, ss_, qn_, r1_, r2_, qb_ = pq[i2], pk[i2], pT[i2], ssum[i2], qn[i2], r1[i2], r2[i2], qb[i2]
                for k in range(8):
                    P.op('pe', lambda e, k=k, t=t, pq_=pq_: e.matmul(pq_[:], lhsT=hT[:, k, t * 128:(t + 1) * 128], rhs=wq[:, k, :],
                                                                    start=(k == 0), stop=(k == 7)),
                         reads=[hT.r, wq.r], writes=[pq_.r])
                for k in range(8):
                    P.op('pe', lambda e, k=k, t=t, pk_=pk_: e.matmul(pk_[:, 0:256], lhsT=hT[:, k, t * 128:(t + 1) * 128], rhs=wkv[:, k, :],
                                                                    start=(k == 0), stop=(k == 7)),
                         reads=[hT.r, wkv.r], writes=[pk_.r])
                yield
                P.op('act', lambda e, t=t, pk_=pk_: e.activation(out=V[:, t, :, 0:64], in_=pk_[:, 128:256].rearrange("p (g d) -> p g d", g=2),
                                                                func=AF.Copy), reads=[pk_.r], writes=[V.r])
                if glb:
                    P.op('act', lambda e, pq_=pq_: e.activation(out=sq[:, 0:512], in_=pq_[:], func=AF.Square), reads=[pq_.r], writes=[sq.r])
                    P.op('act', lambda e, pk_=pk_: e.activation(out=sq[:, 512:640], in_=pk_[:, 0:128], func=AF.Square), reads=[pk_.r], writes=[sq.r])
                    P.op('dve', lambda e, ss_=ss_: e.tensor_reduce(out=ss_[:], in_=sq[:].rearrange("p (h d) -> p h d", d=64), axis=AX.X, op=ALU.add),
                         reads=[sq.r], writes=[ss_.r])
                    P.op('act', lambda e, ss_=ss_: e.activation(out=ss_[:], in_=ss_[:], func=AF.Sqrt, bias=EPS, scale=1.0 / 64),
                         reads=[ss_.r], writes=[ss_.r])
                    P.op('dve', lambda e, ss_=ss_: e.reciprocal(out=ss_[:], in_=ss_[:]), reads=[ss_.r], writes=[ss_.r])
                    P.op('dve', lambda e, pq_=pq_, ss_=ss_, qn_=qn_: e.tensor_tensor(
                        out=qn_[:, 0:512].rearrange("p (h d) -> p h d", d=64), in0=pq_[:].rearrange("p (h d) -> p h d", d=64),
                        in1=ss_[:, 0:8].unsqueeze(2).broadcast_to([128, 8, 64]), op=ALU.mult), reads=[pq_.r, ss_.r], writes=[qn_.r])
                    P.op('dve', lambda e, pk_=pk_, ss_=ss_, qn_=qn_: e.tensor_tensor(
                        out=qn_[:, 512:640].rearrange("p (h d) -> p h d", d=64), in0=pk_[:, 0:128].rearrange("p (h d) -> p h d", d=64),
                        in1=ss_[:, 8:10].unsqueeze(2).broadcast_to([128, 2, 64]), op=ALU.mult), reads=[pk_.r, ss_.r], writes=[qn_.r])
                    P.op('dve', lambda e, qn_=qn_: e.tensor_tensor(
                        out=qn_[:, 0:512].rearrange("p (h d) -> p h d", d=64), in0=qn_[:, 0:512].rearrange("p (h d) -> p h d", d=64),
                        in1=gq[:, 0:1, :].broadcast_to([128, 8, 64]), op=ALU.mult), reads=[qn_.r, gq.r], writes=[qn_.r])
                    P.op('dve', lambda e, qn_=qn_: e.tensor_tensor(
                        out=qn_[:, 512:640].rearrange("p (h d) -> p h d", d=64), in0=qn_[:, 512:640].rearrange("p (h d) -> p h d", d=64),
                        in1=gq[:, 1:2, :].broadcast_to([128, 2, 64]), op=ALU.mult), reads=[qn_.r, gq.r], writes=[qn_.r])
                else:
                    P.op('act', lambda e, pq_=pq_, qn_=qn_: e.activation(out=qn_[:, 0:512], in_=pq_[:], func=AF.Copy), reads=[pq_.r], writes=[qn_.r])
                    P.op('act', lambda e, pk_=pk_, qn_=qn_: e.activation(out=qn_[:, 512:640], in_=pk_[:, 0:128], func=AF.Copy), reads=[pk_.r], writes=[qn_.r])
                q3 = qn_[:].rearrange("p (h d) -> p h d", d=64)
                qo3 = qb_[:].rearrange("p (h d) -> p h d", d=64)
                cosb = C.cosT[:, t:t + 1, :].broadcast_to([128, 10, 32])
                sinb = C.sinT[:, t:t + 1, :].broadcast_to([128, 10, 32])
                rd = [qn_.r, C.cosT.r, C.sinT.r]
                P.op('dve', lambda e, q3=q3, r1_=r1_, cosb=cosb: e.tensor_tensor(out=r1_[:], in0=q3[:, :, 0:32], in1=cosb, op=ALU.mult), reads=rd, writes=[r1_.r])
                P.op('dve', lambda e, q3=q3, r2_=r2_, sinb=sinb: e.tensor_tensor(out=r2_[:], in0=q3[:, :, 32:64], in1=sinb, op=ALU.mult), reads=rd, writes=[r2_.r])
                P.op('dve', lambda e, qo3=qo3, r1_=r1_, r2_=r2_: e.tensor_tensor(out=qo3[:, :, 0:32], in0=r1_[:], in1=r2_[:], op=ALU.subtract),
                     reads=[r1_.r, r2_.r], writes=[qb_.r])
                P.op('dve', lambda e, q3=q3, r1_=r1_, cosb=cosb: e.tensor_tensor(out=r1_[:], in0=q3[:, :, 32:64], in1=cosb, op=ALU.mult), reads=rd, writes=[r1_.r])
                P.op('dve', lambda e, q3=q3, r2_=r2_, sinb=sinb: e.tensor_tensor(out=r2_[:], in0=q3[:, :, 0:32], in1=sinb, op=ALU.mult), reads=rd, writes=[r2_.r])
                P.op('dve', lambda e, qo3=qo3, r1_=r1_, r2_=r2_: e.tensor_tensor(out=qo3[:, :, 32:64], in0=r1_[:], in1=r2_[:], op=ALU.add),
                     reads=[r1_.r, r2_.r], writes=[qb_.r])
                for j in range(5):
                    P.op('pe', lambda e, j=j, qb_=qb_, pT_=pT_: e.transpose(out=pT_[:, j * 128:(j + 1) * 128], in_=qb_[:, j * 128:(j + 1) * 128],
                                                                           identity=C.identb[:]),
                         reads=[qb_.r, C.identb.r], writes=[pT_.r])
                P.op('act', lambda e, t=t, pT_=pT_: e.activation(out=QT[:, :, t * 128:(t + 1) * 128],
                                                                in_=pT_[:, 0:512].rearrange("p (j n) -> p j n", j=4), func=AF.Copy),
                     reads=[pT_.r], writes=[QT.r])
                for g in range(2):
                    P.op('act', lambda e, t=t, pT_=pT_, g=g: e.activation(out=KT[g * 64:(g + 1) * 64, g, t * 128:(t + 1) * 128],
                                                                         in_=pT_[g * 64:(g + 1) * 64, 512:640], func=AF.Copy),
                         reads=[pT_.r], writes=[KT.r])
            gens1 = {}
            for t in range(NT + 1):
                if t < NT:
                    gens1[t] = tile1(t)
                    next(gens1[t])
                if t >= 1:
                    next(gens1.pop(t - 1), None)
        P.barrier()
        with ExitStack() as es2:
            NB = 2
            LA = 2
            psS = [mk(es2, f"apsS{i}", [128, NB, 512], F32, psum=True) for i in range(3)]
            psO = [mk(es2, f"apsO{i}", [128, 512], F32, psum=True) for i in range(1)]
            psB = mk(es2, "apsB", [128, 512], F32, psum=True)
            pt = [mk(es2, f"apt{i}", [128, NB, 512], BF16) for i in range(4)]
            den = [mk(es2, f"aden{i}", [128, 512], F32) for i in range(3)]
            osb = [mk(es2, f"aosb{i}", [64, 512], F32) for i in range(3)]
            obf = [mk(es2, f"aobf{i}", [64, 4, 128], BF16) for i in range(3)]
            pending = []
            batches = []
            for qb_i in range(NT):
                if qb_i < 2:
                    kts = [(0, None), (1, None)]
                elif glb:
                    kts = [(k, None) for k in range(NT)]
                else:
                    kts = [(0, None), (1, None)]
                    if qb_i - 1 >= 2:
                        kts.append((qb_i - 1, 0))
                    kts.append((qb_i, None))
                    if qb_i + 1 < NT:
                        kts.append((qb_i + 1, 1))
                for g in range(2):
                    nb = (len(kts) + NB - 1) // NB
                    for bi in range(nb):
                        batches.append((qb_i, g, kts[bi * NB:(bi + 1) * NB], bi == 0, bi == nb - 1))
            BRv = C.BR[br].rearrange("(two d) c t -> d c two t", two=2)
            grp = 0

            def do_pv(i):
                nonlocal grp
                qb_i, g, kts, first, last = batches[i]
                o_ = psO[0]
                p_ = pt[i % 4]
                for j, (kt, m) in enumerate(kts):
                    P.op('pe', lambda e, o_=o_, p_=p_, kt=kt, g=g, j=j, st=(first and j == 0), sp=(last and j == len(kts) - 1): e.matmul(
                        o_[:, :], lhsT=V[:, kt, g, :], rhs=p_[:, j, :], start=st, stop=sp),
                        reads=[V.r, p_.r], writes=[o_.r])
                if last:
                    d_, s_, b_ = den[grp % 3], osb[grp % 3], obf[grp % 3]
                    while pending:
                        pending.pop(0)()
                    if glb:
                        P.op('dve', lambda e, o_=o_, d_=d_: e.tensor_copy(out=d_[64:65, :], in_=o_[64:65, :]), reads=[o_.r], writes=[d_.r])
                    else:
                        P.op('dve', lambda e, o_=o_, d_=d_, g=g: e.tensor_tensor(
                            out=d_[64:65, :].rearrange("p (j n) -> p j n", j=4), in0=o_[64:65, :].rearrange("p (j n) -> p j n", j=4),
                            in1=esink[64:65, 4 * g:4 * g + 4].unsqueeze(2).broadcast_to([1, 4, 128]), op=ALU.add),
                            reads=[o_.r, esink.r], writes=[d_.r])
                        P.op('act', lambda e, d_=d_: e.activation(out=d_[64:65, :], in_=d_[64:65, :], func=AF.Ln), reads=[d_.r], writes=[d_.r])
                        P.op('act', lambda e, d_=d_: e.activation(out=d_[64:65, :], in_=d_[64:65, :], func=AF.Exp, scale=-1.0), reads=[d_.r], writes=[d_.r])
                    P.op('dve', lambda e, o_=o_, s_=s_: e.tensor_copy(out=s_[:], in_=o_[0:64, :]), reads=[o_.r], writes=[s_.r])
                    if glb:
                        P.op('dve', lambda e, d_=d_: e.reciprocal(out=d_[64:65, :], in_=d_[64:65, :]), reads=[d_.r], writes=[d_.r])

                    def fin(d_=d_, s_=s_, b_=b_, g=g, qb_i=qb_i):
                        P.op('pe', lambda e: e.matmul(psB[0:64, :], lhsT=C.onesr[64:65, :], rhs=d_[64:65, :], start=True, stop=True),
                             reads=[C.onesr.r, d_.r], writes=[psB.r])
                        P.op('dve', lambda e: e.tensor_tensor(out=b_[:].rearrange("p j n -> p (j n)"), in0=s_[:], in1=psB[0:64, :], op=ALU.mult),
                             reads=[s_.r, psB.r], writes=[b_.r])
                        P.dma('sp', [lambda e, cl=cl: e.dma_start(
                            out=BRv[:, 2 * g + cl, :, qb_i * 128:(qb_i + 1) * 128],
                            in_=b_[:, 2 * cl:2 * cl + 2, :]) for cl in range(2)], reads=[b_.r], writes=[C.rBR])
                    pending.append(fin)
                    grp += 1

            for i in range(len(batches) + LA):
                if i < len(batches):
                    qb_i, g, kts, first, last = batches[i]
                    s_ = psS[i % 3]
                    p_ = pt[i % 4]
                    n = len(kts)
                    for j, (kt, m) in enumerate(kts):
                        P.op('pe', lambda e, s_=s_, kt=kt, g=g, qb_i=qb_i, j=j: e.matmul(
                            s_[:, j, :].rearrange("p (j n) -> p j n", j=4), lhsT=KT[:, g, kt * 128:(kt + 1) * 128],
                            rhs=QT[:, :, qb_i * 128:(qb_i + 1) * 128], start=True, stop=True),
                            reads=[KT.r, QT.r], writes=[s_.r])
                    P.op('act', lambda e, s_=s_, p_=p_, n=n: e.activation(out=p_[:, 0:n, :], in_=s_[:, 0:n, :], func=AF.Exp, scale=0.125),
                         reads=[s_.r], writes=[p_.r])
                    for j, (kt, m) in enumerate(kts):
                        if m is not None:
                            P.op('dve', lambda e, p_=p_, m=m, j=j: e.tensor_tensor(out=p_[:, j, :], in0=p_[:, j, :], in1=C.maskw[:, m, :], op=ALU.mult),
                                 reads=[p_.r, C.maskw.r], writes=[p_.r])
                if i >= LA:
                    do_pv(i - LA)
            while pending:
                pending.pop(0)()
        P.barrier()
        P.emit()


def phase_ret(C, l, hT):
    for hh in range(2):
        _ret_half(C, l, hT, hh)


def _ret_half(C, l, hT, hh):
    P, nc = C.P, C.nc
    mk = C.mk
    col0 = 1024 + 768
    cq, ck, cv, cg = col0 + hh * 128, col0 + 256 + hh * 128, col0 + 512 + hh * 256, col0 + 1024 + hh * 256
    with ExitStack() as es:
        wqk = mk(es, "rwqk", [128, 8, 256], BF16)
        wv = mk(es, "rwv", [128, 8, 256], BF16)
        wg = mk(es, "rwg", [128, 8, 256], BF16)
        kd = mk(es, "rkd", [128, NT, 256], BF16)
        vb = mk(es, "rvb", [128, NT, 256], BF16)
        Sall = mk(es, "rSall", [128, NT, 256], BF16)
        gsb = mk(es, "rgsb", [128, NT, 256], BF16)
        S = mk(es, "rS", [128, 256], F32)
        lg = mk(es, "rlg", [128, 8], F32)
        dq = mk(es, "rdq", [128, 2, 2], F32)
        dk = mk(es, "rdk", [128, 2, 2], F32)
        cd = mk(es, "rcd", [128, 2, 128], F32)
        Dm = mk(es, "rDm", [128, 2, 128], F32)
        tmpd = mk(es, "rtmpd", [128, 128], F32)
        gng = mk(es, "rgng", [128, 256], F32)
        c128 = mk(es, "rc128", [128, 1], F32)
        load_w(C, wqk, C.w_in[l], cq, 128, d0=0)
        load_w(C, wqk, C.w_in[l], ck, 128, d0=128)
        load_w(C, wv, C.w_in[l], cv, 256)
        load_w(C, wg, C.w_in[l], cg, 256)
        P.dma('sp', [lambda e: e.dma_start(out=lg[:], in_=C.c_decay[l].partition_broadcast(128)),
                     lambda e: e.dma_start(out=gng[:], in_=C.c_gn_g[l, hh * 256:(hh + 1) * 256].partition_broadcast(128))],
              writes=[lg.r, gng.r])
        P.op('act', lambda e: e.activation(out=lg[:], in_=lg[:], func=AF.Exp, scale=-1.0), reads=[lg.r], writes=[lg.r])
        P.op('act', lambda e: e.activation(out=lg[:], in_=lg[:], func=AF.Ln, bias=1.0, scale=1.0), reads=[lg.r], writes=[lg.r])
        P.op('dve', lambda e: e.tensor_scalar(out=lg[:], in0=lg[:], scalar1=-1.0, scalar2=None, op0=ALU.mult), reads=[lg.r], writes=[lg.r])
        rc = C.retc
        P.op('dve', lambda e: e.tensor_tensor(out=c128[:], in0=rc[:, 0:1], in1=rc[:, 2:3], op=ALU.add), reads=[rc.r], writes=[c128.r])
        for h2 in range(2):
            h = 2 * hh + h2
            P.op('act', lambda e, h=h, h2=h2: e.activation(out=dq[:, h2, 0:1], in_=rc[:, 0:1], func=AF.Exp, scale=lg[:, h:h + 1]), reads=[rc.r, lg.r], writes=[dq.r])
            P.op('act', lambda e, h=h, h2=h2: e.activation(out=dq[:, h2, 1:2], in_=rc[:, 1:2], func=AF.Exp, scale=lg[:, 4 + h:5 + h]), reads=[rc.r, lg.r], writes=[dq.r])
            P.op('act', lambda e, h=h, h2=h2: e.activation(out=dk[:, h2, 0:1], in_=rc[:, 2:3], func=AF.Exp, scale=lg[:, h:h + 1]), reads=[rc.r, lg.r], writes=[dk.r])
            P.op('act', lambda e, h=h, h2=h2: e.activation(out=dk[:, h2, 1:2], in_=rc[:, 3:4], func=AF.Exp, scale=lg[:, 4 + h:5 + h]), reads=[rc.r, lg.r], writes=[dk.r])
            for (r0, r1_, off) in ((0, 64, 0), (64, 128, 4)):
                P.op('act', lambda e, h=h, h2=h2, r0=r0, r1_=r1_, off=off: e.activation(
                    out=cd[r0:r1_, h2, 0:1], in_=c128[r0:r1_, :], func=AF.Exp, scale=lg[r0:r1_, off + h:off + h + 1]),
                    reads=[c128.r, lg.r], writes=[cd.r])
            P.op('act', lambda e, h=h: e.activation(out=tmpd[:], in_=rc[:, 4:132], func=AF.Exp, scale=lg[:, h:h + 1]), reads=[rc.r, lg.r], writes=[tmpd.r])
            P.op('dve', lambda e, h2=h2: e.tensor_tensor(out=Dm[:, h2, :], in0=tmpd[:], in1=rc[:, 260:388], op=ALU.mult), reads=[tmpd.r, rc.r], writes=[Dm.r])
            P.op('act', lambda e, h=h: e.activation(out=tmpd[:], in_=rc[:, 132:260], func=AF.Exp, scale=lg[:, 4 + h:5 + h]), reads=[rc.r, lg.r, Dm.r], writes=[tmpd.r])
            P.op('dve', lambda e: e.tensor_tensor(out=tmpd[:], in0=tmpd[:], in1=rc[:, 388:516], op=ALU.mult), reads=[tmpd.r, rc.r], writes=[tmpd.r])
            P.op('dve', lambda e, h2=h2: e.tensor_tensor(out=Dm[:, h2, :], in0=Dm[:, h2, :], in1=tmpd[:], op=ALU.add), reads=[tmpd.r, Dm.r], writes=[Dm.r])
        P.op('dve', lambda e: e.tensor_scalar(out=Dm[:], in0=Dm[:], scalar1=0.125, scalar2=None, op0=ALU.mult), reads=[Dm.r], writes=[Dm.r])
        P.op('dve', lambda e: e.tensor_scalar(out=dk[:], in0=dk[:], scalar1=0.125, scalar2=None, op0=ALU.mult), reads=[dk.r], writes=[dk.r])
        P.op('dve', lambda e: e.tensor_copy(out=cd[:, :, 1:128], in_=cd[:, :, 0:1].broadcast_to([128, 2, 127])), reads=[cd.r], writes=[cd.r])
        P.op('dve', lambda e: e.memset(S[:], 0.0), writes=[S.r])
        cdf = cd[:].rearrange("p h n -> p (h n)")
        import os
        RS = os.environ.get('RET_STOP', '')
        if RS == 'setup':
            P.barrier(); P.emit(); return
        with ExitStack() as es1:
            pk = [mk(es1, f"rpk{i}", [128, 512], F32, psum=True) for i in range(2)]
            pv = [mk(es1, f"rpv{i}", [128, 512], F32, psum=True) for i in range(2)]
            pU = [mk(es1, f"rpU{i}", [128, 512], F32, psum=True) for i in range(2)]
            pG = [mk(es1, f"rpG{i}", [128, 512], F32, psum=True) for i in range(2)]

            def u_mm(t, pU_):
                for h2 in range(2):
                    hs = slice(h2 * 128, (h2 + 1) * 128)
                    P.op('pe', lambda e, hs=hs, t=t, pU_=pU_: e.matmul(pU_[:, hs], lhsT=kd[:, t, hs], rhs=vb[:, t, hs], start=True, stop=True),
                         reads=[kd.r, vb.r], writes=[pU_.r])

            def chain(t, pU_, r0, r1_):
                P.op('dve', lambda e: e.tensor_copy(out=Sall[r0:r1_, t, :], in_=S[r0:r1_, :]), reads=[S.r], writes=[Sall.r])
                P.op('dve', lambda e: e.tensor_tensor(out=S[r0:r1_, :], in0=S[r0:r1_, :], in1=cdf[r0:r1_, :], op=ALU.mult),
                     reads=[S.r, cd.r], writes=[S.r])
                P.op('dve', lambda e: e.tensor_tensor(out=S[r0:r1_, :], in0=S[r0:r1_, :], in1=pU_[r0:r1_, 0:256], op=ALU.add),
                     reads=[S.r, pU_.r], writes=[S.r])

            for t in range(NT):
                i2 = t % 2
                pk_, pv_, pU_ = pk[i2], pv[i2], pU[i2]
                tk = slice(t * 128, (t + 1) * 128)
                for k in range(8):
                    P.op('pe', lambda e, k=k, tk=tk, pk_=pk_: e.matmul(pk_[:, 0:128], lhsT=hT[:, k, tk], rhs=wqk[:, k, 128:256], start=(k == 0), stop=(k == 7)),
                         reads=[hT.r, wqk.r], writes=[pk_.r])
                for k in range(8):
                    P.op('pe', lambda e, k=k, tk=tk, pv_=pv_: e.matmul(pv_[:, 0:256], lhsT=hT[:, k, tk], rhs=wv[:, k, :], start=(k == 0), stop=(k == 7)),
                         reads=[hT.r, wv.r], writes=[pv_.r])
                pG_ = pG[i2]
                for k in range(8):
                    P.op('pe', lambda e, k=k, tk=tk, pG_=pG_: e.matmul(pG_[:, 0:256], lhsT=hT[:, k, tk], rhs=wg[:, k, :], start=(k == 0), stop=(k == 7)),
                         reads=[hT.r, wg.r], writes=[pG_.r])
                P.op('act', lambda e, t=t, pv_=pv_: e.activation(out=vb[:, t, :], in_=pv_[:, 0:256], func=AF.Copy), reads=[pv_.r], writes=[vb.r])
                P.op('act', lambda e, t=t, pG_=pG_: e.activation(out=gsb[:, t, :], in_=pG_[:, 0:256], func=AF.Silu), reads=[pG_.r], writes=[gsb.r])
                P.op('pool', lambda e, t=t: e.tensor_tensor(out=gsb[:, t, :], in0=gsb[:, t, :], in1=gng[:], op=ALU.mult), reads=[gsb.r, gng.r], writes=[gsb.r])
                P.op('dve', lambda e, t=t, pk_=pk_: e.tensor_tensor(
                    out=kd[:, t, :].rearrange("p (h a d) -> p h a d", h=2, a=2),
                    in0=pk_[:, 0:128].rearrange("p (h d) -> p h d", h=2).unsqueeze(2).broadcast_to([128, 2, 2, 64]),
                    in1=dk[:].unsqueeze(3).broadcast_to([128, 2, 2, 64]), op=ALU.mult), reads=[pk_.r, dk.r], writes=[kd.r])
                u_mm(t, pU_)
                chain(t, pU_, 0, 64)
            order = [1, 0] + list(range(NT - 1, 1, -1))
            for i, t in enumerate(order):
                pU_ = pU[i % 2]
                u_mm(t, pU_)
                chain(t, pU_, 64, 128)
        P.barrier()
        if RS == 'pass1':
            P.barrier(); P.emit(); return
        with ExitStack() as es2:
            pqk = mk(es2, "rpqk", [128, 512], F32, psum=True)
            pT = mk(es2, "rpT", [128, 1024], BF16, psum=True)
            pSG = [mk(es2, f"rpSG{i}", [128, 512], F32, psum=True) for i in range(2)]
            pO = [mk(es2, f"rpO{i}", [128, 512], F32, psum=True) for i in range(2)]
            pT2 = mk(es2, "rpT2", [128, 1024], BF16, psum=True)
            qkb = [mk(es2, f"rqkb{i}", [128, 256], BF16) for i in range(2)]
            qdb = [mk(es2, f"rqdb{i}", [128, 256], BF16) for i in range(2)]
            qkT = [mk(es2, f"rqkT{i}", [128, 2, 128], BF16) for i in range(2)]
            qdT = [mk(es2, f"rqdT{i}", [128, 2, 128], BF16) for i in range(2)]
            pTb = [mk(es2, f"rpTb{i}", [128, 256], BF16) for i in range(2)]
            st = [mk(es2, f"rst{i}", [128, 2, 6], F32) for i in range(2)]
            mv = [mk(es2, f"rmv{i}", [128, 2, 2], F32) for i in range(2)]
            on = [mk(es2, f"ron{i}", [128, 256], F32) for i in range(2)]
            gs = [mk(es2, f"rgs{i}", [128, 256], F32) for i in range(2)]
            sqb = mk(es2, "rsqb", [128, 256], F32)
            yb = [mk(es2, f"ryb{i}", [128, 256], BF16) for i in range(2)]
            yT = [mk(es2, f"ryT{i}", [128, 2, 128], BF16) for i in range(2)]
            def chunk(t):
                i2 = t % 2
                pS_, pO_, qkb_, qdb_, qkT_, qdT_, pTb_, st_, mv_, on_, gs_, yb_, yT_ = (
                    pSG[0], pO[i2], qkb[i2], qdb[i2], qkT[i2], qdT[i2], pTb[i2], st[i2], mv[i2], on[i2], gs[i2], yb[i2], yT[i2])
                tk = slice(t * 128, (t + 1) * 128)
                for k in range(8):
                    P.op('pe', lambda e, k=k, tk=tk: e.matmul(pqk[:, 0:256], lhsT=hT[:, k, tk], rhs=wqk[:, k, :], start=(k == 0), stop=(k == 7)),
                         reads=[hT.r, wqk.r], writes=[pqk.r])
                P.op('act', lambda e, qkb_=qkb_: e.activation(out=qkb_[:], in_=pqk[:, 0:256], func=AF.Copy), reads=[pqk.r], writes=[qkb_.r])
                P.op('dve', lambda e, qdb_=qdb_: e.tensor_tensor(
                    out=qdb_[:].rearrange("p (h a d) -> p h a d", h=2, a=2),
                    in0=pqk[:, 0:128].rearrange("p (h d) -> p h d", h=2).unsqueeze(2).broadcast_to([128, 2, 2, 64]),
                    in1=dq[:].unsqueeze(3).broadcast_to([128, 2, 2, 64]), op=ALU.mult), reads=[pqk.r, dq.r], writes=[qdb_.r])
                for j in range(2):
                    P.op('pe', lambda e, j=j, qkb_=qkb_: e.transpose(out=pT[:, j * 128:(j + 1) * 128], in_=qkb_[:, j * 128:(j + 1) * 128],
                                                                    identity=C.identb[:]), reads=[qkb_.r, C.identb.r], writes=[pT.r])
                for j in range(2):
                    P.op('pe', lambda e, j=j, qdb_=qdb_: e.transpose(out=pT[:, 256 + j * 128:256 + (j + 1) * 128], in_=qdb_[:, j * 128:(j + 1) * 128],
                                                                    identity=C.identb[:]), reads=[qdb_.r, C.identb.r], writes=[pT.r])
                P.op('act', lambda e, qkT_=qkT_: e.activation(out=qkT_[:], in_=pT[:, 0:256].rearrange("p (j n) -> p j n", j=2), func=AF.Copy),
                     reads=[pT.r], writes=[qkT_.r])
                P.op('dve', lambda e, qdT_=qdT_: e.tensor_copy(out=qdT_[:], in_=pT[:, 256:512].rearrange("p (j n) -> p j n", j=2)),
                     reads=[pT.r], writes=[qdT_.r])
                for h2 in range(2):
                    b0 = h2 * 64
                    P.op('pe', lambda e, h2=h2, b0=b0, qkT_=qkT_: e.matmul(pSG[h2][:, 0:128], lhsT=qkT_[b0:b0 + 64, 1, :],
                                                                          rhs=qkT_[b0:b0 + 64, 0, :], start=True, stop=True),
                         reads=[qkT_.r], writes=[pSG[h2].r])
                for h2 in range(2):
                    P.op('dve', lambda e, h2=h2, pTb_=pTb_: e.tensor_tensor(out=pTb_[:, h2 * 128:(h2 + 1) * 128], in0=pSG[h2][:, 0:128], in1=Dm[:, h2, :], op=ALU.mult),
                         reads=[pSG[h2].r, Dm.r], writes=[pTb_.r])
                for h2 in range(2):
                    hs = slice(h2 * 128, (h2 + 1) * 128)
                    P.op('pe', lambda e, hs=hs, t=t, pO_=pO_, pTb_=pTb_: e.matmul(pO_[:, hs], lhsT=pTb_[:, hs], rhs=vb[:, t, hs], start=True, stop=False),
                         reads=[pTb_.r, vb.r], writes=[pO_.r])
                    P.op('pe', lambda e, h2=h2, hs=hs, t=t, pO_=pO_, qdT_=qdT_: e.matmul(pO_[:, hs], lhsT=qdT_[:, h2, :], rhs=Sall[:, t, hs], start=False, stop=True),
                         reads=[qdT_.r, Sall.r], writes=[pO_.r])
                yield
                P.op('act', lambda e, pO_=pO_, on_=on_: e.activation(out=on_[:], in_=pO_[:, 0:256], func=AF.Copy), reads=[pO_.r], writes=[on_.r])
                P.op('act', lambda e, pO_=pO_: e.activation(out=sqb[:], in_=pO_[:, 0:256], func=AF.Square), reads=[pO_.r], writes=[sqb.r])
                P.op('dve', lambda e, on_=on_, st_=st_: e.tensor_reduce(out=st_[:, 0, 0:2], in_=on_[:].rearrange("p (h d) -> p h d", h=2), axis=AX.X, op=ALU.add),
                     reads=[on_.r], writes=[st_.r])
                P.op('dve', lambda e, st_=st_: e.tensor_reduce(out=st_[:, 0, 2:4], in_=sqb[:].rearrange("p (h d) -> p h d", h=2), axis=AX.X, op=ALU.add),
                     reads=[sqb.r], writes=[st_.r])
                P.op('dve', lambda e, st_=st_, mv_=mv_: e.tensor_scalar(out=mv_[:, 0, :], in0=st_[:, 0, 0:2], scalar1=1.0 / 128, scalar2=None, op0=ALU.mult),
                     reads=[st_.r], writes=[mv_.r])
                P.op('dve', lambda e, st_=st_, mv_=mv_: e.tensor_tensor(out=st_[:, 1, 0:2], in0=mv_[:, 0, :], in1=mv_[:, 0, :], op=ALU.mult),
                     reads=[mv_.r], writes=[st_.r])
                P.op('dve', lambda e, st_=st_, mv_=mv_: e.scalar_tensor_tensor(out=mv_[:, 1, :], in0=st_[:, 0, 2:4], scalar=1.0 / 128, in1=st_[:, 1, 0:2],
                                                                               op0=ALU.mult, op1=ALU.subtract), reads=[st_.r], writes=[mv_.r])
                P.op('act', lambda e, mv_=mv_: e.activation(out=mv_[:, 1, :], in_=mv_[:, 1, :], func=AF.Sqrt, bias=EPS, scale=1.0), reads=[mv_.r], writes=[mv_.r])
                P.op('dve', lambda e, mv_=mv_: e.reciprocal(out=mv_[:, 1, :], in_=mv_[:, 1, :]), reads=[mv_.r], writes=[mv_.r])
                for h2 in range(2):
                    hs = slice(h2 * 128, (h2 + 1) * 128)
                    P.op('dve', lambda e, h2=h2, hs=hs, on_=on_, mv_=mv_: e.tensor_scalar(
                        out=on_[:, hs], in0=on_[:, hs], scalar1=mv_[:, 0, h2:h2 + 1], scalar2=mv_[:, 1, h2:h2 + 1], op0=ALU.subtract, op1=ALU.mult),
                        reads=[on_.r, mv_.r], writes=[on_.r])
                P.op('dve', lambda e, on_=on_, t=t, yb_=yb_: e.tensor_tensor(out=yb_[:], in0=on_[:], in1=gsb[:, t, :], op=ALU.mult),
                     reads=[on_.r, gsb.r], writes=[yb_.r])
                for j in range(2):
                    P.op('pe', lambda e, j=j, yb_=yb_: e.transpose(out=pT2[:, j * 128:(j + 1) * 128], in_=yb_[:, j * 128:(j + 1) * 128],
                                                                  identity=C.identb[:]), reads=[yb_.r, C.identb.r], writes=[pT2.r])
                P.op('act', lambda e, yT_=yT_: e.activation(out=yT_[:], in_=pT2[:, 0:256].rearrange("p (j n) -> p j n", j=2), func=AF.Copy),
                     reads=[pT2.r], writes=[yT_.r])
                P.dma('sp', lambda e, tk=tk, yT_=yT_: e.dma_start(out=C.BR[2, :, 2 * hh:2 * hh + 2, tk], in_=yT_[:]), reads=[yT_.r], writes=[C.rBR])
            gens = {}
            for t in range(NT + 1):
                if t < NT:
                    gens[t] = chunk(t)
                    next(gens[t])
                if t >= 1:
                    next(gens.pop(t - 1), None)
        P.barrier()
        P.emit()


def resid_epilogue(C, es, name, l, sub):
    mk = C.mk
    st = Ctx()
    st.mod = load_mod_tiles(C, es, l, sub, 'G')
    st.xt = [mk(es, f"{name}xt{i}", [128, D], F32) for i in range(2)]
    st.sq = mk(es, f"{name}sq", [128, 512], BF16)
    st.ss = [mk(es, f"{name}ss{i}", [128, 4], F32) for i in range(2)]
    st.tmp = [mk(es, f"{name}tmp{i}", [128, D], F32) for i in range(2)]
    st.n = 0

    def run(t, py0, py1):
        P = C.P
        i2 = st.n % 2
        st.n += 1
        r = 1 if t < 2 else 0
        x_, s_, t_ = st.xt[i2], st.ss[i2], st.tmp[i2]
        P.dma('sp', lambda e: e.dma_start(out=x_[:], in_=C.X[t * 128:(t + 1) * 128, :]), reads=[C.rX], writes=[x_.r])
        P.op('act', lambda e: e.activation(out=st.sq[:], in_=py0[:], func=AF.Square, scale=1.0 / 32, accum_out=s_[:, 0:1]),
             reads=[py0.r], writes=[st.sq.r, s_.r])
        P.op('act', lambda e: e.activation(out=st.sq[:], in_=py1[:], func=AF.Square, scale=1.0 / 32, accum_out=s_[:, 1:2]),
             reads=[py1.r], writes=[st.sq.r, s_.r])
        P.op('dve', lambda e: e.tensor_tensor(out=s_[:, 2:3], in0=s_[:, 0:1], in1=s_[:, 1:2], op=ALU.add), reads=[s_.r], writes=[s_.r])
        P.op('act', lambda e: e.activation(out=s_[:, 3:4], in_=s_[:, 2:3], func=AF.Sqrt, bias=EPS, scale=1.0), reads=[s_.r], writes=[s_.r])
        P.op('dve', lambda e: e.reciprocal(out=s_[:, 3:4], in_=s_[:, 3:4]), reads=[s_.r], writes=[s_.r])
        G = st.mod[r][0]
        for half, py in ((0, py0), (1, py1)):
            P.op('dve', lambda e, half=half, py=py: e.scalar_tensor_tensor(
                out=t_[:, half * 512:(half + 1) * 512], in0=py[:], scalar=s_[:, 3:4], in1=G[:, half * 512:(half + 1) * 512],
                op0=ALU.mult, op1=ALU.mult), reads=[py.r, s_.r, G.r], writes=[t_.r])
        P.op('pool', lambda e: e.tensor_tensor(out=t_[:], in0=t_[:], in1=x_[:], op=ALU.add), reads=[t_.r, x_.r], writes=[t_.r])
        P.dma('pool', lambda e: e.dma_start(out=C.X[t * 128:(t + 1) * 128, :], in_=t_[:]), reads=[t_.r], writes=[C.rX])
    return run


def phase_merge(C, l):
    P, nc = C.P, C.nc
    mk = C.mk
    with ExitStack() as es:
        wgt = mk(es, "mwg", [128, 8, 4096], BF16)
        wbr = mk(es, "mwbr", [128, 16, D], BF16)
        wo = mk(es, "mwo", [128, 8, D], BF16)
        load_w(C, wgt, C.w_in[l], 4096, 4096)
        load_w(C, wbr, C.w_br[l].rearrange("b c d -> (b c) d"), 0, D, nk=16)
        load_w(C, wo, C.w_o[l], 0, D)
        hTt = [mk(es, f"mhT{i}", [128, 8, 256], BF16) for i in range(2)]
        brt = [mk(es, f"mbr{i}", [128, 4, 4, 256], BF16) for i in range(2)]
        sgm = [mk(es, f"msg{i}", [128, 4, 256], F32) for i in range(2)]
        prod = [mk(es, f"mprod{i}", [128, 4, 256], F32) for i in range(2)]
        accT = [mk(es, f"macc{i}", [128, 8, 256], BF16) for i in range(2)]
        pg = [[mk(es, f"mpg{i}{j}", [128, 512], F32, psum=True) for j in range(2)] for i in range(1)]
        pb = [[mk(es, f"mpb{i}{j}", [128, 512], F32, psum=True) for j in range(2)] for i in range(1)]
        py = [mk(es, f"mpy{i}", [128, 512], F32, psum=True) for i in range(4)]
        epi = resid_epilogue(C, es, "me", l, 0)
        def mload(tt_):
            t0 = tt_ * 256
            h_, b_ = hTt[tt_ % 2], brt[tt_ % 2]
            P.dma('sp', lambda e, h_=h_, t0=t0: e.dma_start(out=h_[:], in_=C.HTd[:, :, t0:t0 + 256]), reads=[C.rHTd], writes=[h_.r])
            P.dma('sp', [lambda e, b_=b_, t0=t0, i=i: e.dma_start(out=b_[:, i], in_=C.BR[i, :, :, t0:t0 + 256]) for i in range(4)],
                  reads=[C.rBR], writes=[b_.r])

        mload(0)
        for tt_ in range(T // 256):
            t0 = tt_ * 256
            i2 = tt_ % 2
            h_, b_, a_ = hTt[i2], brt[i2], accT[i2]
            if tt_ + 1 < T // 256:
                mload(tt_ + 1)
            for dc in range(8):
                j2 = dc % 2
                s_, p_ = sgm[j2], prod[j2]
                ds_ = slice(dc * 128, (dc + 1) * 128)
                for hh in range(2):
                  for i in (2 * hh, 2 * hh + 1):
                    pg_ = pg[0][i // 2]
                    for k in range(8):
                        P.op('pe', lambda e, i=i, k=k, pg_=pg_, h_=h_, dc=dc: e.matmul(
                            pg_[:, (i % 2) * 256:(i % 2 + 1) * 256], lhsT=wgt[:, k, i * 1024 + dc * 128:i * 1024 + (dc + 1) * 128],
                            rhs=h_[:, k, :], start=(k == 0), stop=(k == 7)), reads=[wgt.r, h_.r], writes=[pg_.r])
                  for i in (2 * hh, 2 * hh + 1):
                    pb_ = pb[0][i // 2]
                    for k in range(4):
                        P.op('pe', lambda e, i=i, k=k, pb_=pb_, b_=b_, ds_=ds_: e.matmul(
                            pb_[:, (i % 2) * 256:(i % 2 + 1) * 256], lhsT=wbr[:, i * 4 + k, ds_], rhs=b_[:, i, k, :],
                            start=(k == 0), stop=(k == 3)), reads=[wbr.r, b_.r], writes=[pb_.r])
                for hh in range(2):
                    P.op('act', lambda e, hh=hh, s_=s_: e.activation(out=s_[:, 2 * hh:2 * hh + 2, :].rearrange("p a n -> p (a n)"), in_=pg[0][hh][:],
                                                                    func=AF.Sigmoid), reads=[pg[0][hh].r], writes=[s_.r])
                    P.op('dve', lambda e, hh=hh, s_=s_, p_=p_: e.tensor_tensor(
                        out=p_[:, 2 * hh:2 * hh + 2, :].rearrange("p a n -> p (a n)"), in0=s_[:, 2 * hh:2 * hh + 2, :].rearrange("p a n -> p (a n)"),
                        in1=pb[0][hh][:], op=ALU.mult), reads=[s_.r, pb[0][hh].r], writes=[p_.r])
                P.op('dve', lambda e, p_=p_: e.tensor_tensor(out=p_[:, 0:2, :], in0=p_[:, 0:2, :], in1=p_[:, 2:4, :], op=ALU.add), reads=[p_.r], writes=[p_.r])
                P.op('dve', lambda e, p_=p_, a_=a_, dc=dc: e.tensor_tensor(out=a_[:, dc, :], in0=p_[:, 0, :], in1=p_[:, 1, :], op=ALU.add),
                     reads=[p_.r], writes=[a_.r])
            for sub in range(2):
                py0, py1 = py[2 * sub], py[2 * sub + 1]
                for half, pyh in ((0, py0), (1, py1)):
                    for k in range(8):
                        P.op('pe', lambda e, k=k, half=half, pyh=pyh, a_=a_, sub=sub: e.matmul(
                            pyh[:], lhsT=a_[:, k, sub * 128:(sub + 1) * 128], rhs=wo[:, k, half * 512:(half + 1) * 512],
                            start=(k == 0), stop=(k == 7)), reads=[a_.r, wo.r], writes=[pyh.r])
                epi(tt_ * 2 + sub, py0, py1)
        P.barrier()
        P.emit()


def phase_ffn_up(C, l, hT):
    P, nc = C.P, C.nc
    mk = C.mk
    TOK = [(0, 256)] + [(256 + 512 * i, 512) for i in range(8)]
    with ExitStack() as es:
        wv = [mk(es, f"fw{i}", [128, 8, 256], BF16) for i in range(3)]
        dwf = mk(es, "fdwf", [128, 2 * NFC, 3], F32)
        dwb = mk(es, "fdwb", [128, 2 * NFC], F32)
        diag = [mk(es, f"fdiag{i}", [128, 6, 128], BF16) for i in range(2)]
        ub = [mk(es, f"fub{i}", [128, 2, T + 4], BF16) for i in range(2)]
        sa = [mk(es, f"fsa{i}", [128, 512], F32) for i in range(2)]
        mo = [mk(es, f"fmo{i}", [128, 512], BF16) for i in range(3)]
        pu = [mk(es, f"fpu{i}", [128, 512], F32, psum=True) for i in range(4)]
        pc = [mk(es, f"fpc{i}", [128, 512], F32, psum=True) for i in range(4)]

        def uoff(t0):
            return t0 + 1 if t0 < 256 else t0 + 3

        P.dma('sp', [lambda e: e.dma_start(out=dwf[:], in_=C.f_dw[l]), lambda e: e.dma_start(out=dwb[:], in_=C.f_dw_b[l])],
              writes=[dwf.r, dwb.r])
        for i in range(2):
            P.op('dve', lambda e, i=i: e.memset(ub[i][:], 0.0), writes=[ub[i].r])
        it = 0
        im = 0
        for c in range(NFC):
            w_, dg_, u_ = wv[c % 3], diag[c % 2], ub[c % 2]
            load_w(C, w_, C.f_up[l], c * 256, 256)
            for half in range(2):
                for j in range(3):
                    P.op('dve', lambda e, half=half, j=j, c=c, dg_=dg_: e.tensor_scalar(
                        out=dg_[:, half * 3 + j, :], in0=C.identf[:], scalar1=dwf[:, 2 * c + half, j:j + 1], scalar2=None, op0=ALU.mult),
                        reads=[C.identf.r, dwf.r], writes=[dg_.r])
            for (t0, n) in TOK:
                for half in range(2):
                    p_ = pu[it % 4]
                    it += 1
                    for k in range(8):
                        P.op('pe', lambda e, k=k, p_=p_, half=half, w_=w_, t0=t0, n=n: e.matmul(
                            p_[:, 0:n], lhsT=w_[:, k, half * 128:(half + 1) * 128], rhs=hT[:, k, t0:t0 + n],
                            start=(k == 0), stop=(k == 7)), reads=[w_.r, hT.r], writes=[p_.r])
                    if half == 0:
                        P.op('act', lambda e, p_=p_, u_=u_, t0=t0, n=n: e.activation(out=u_[:, 0, uoff(t0):uoff(t0) + n], in_=p_[:, 0:n], func=AF.Copy),
                             reads=[p_.r], writes=[u_.r])
                    else:
                        P.op('dve', lambda e, p_=p_, u_=u_, t0=t0, n=n: e.tensor_copy(out=u_[:, 1, uoff(t0):uoff(t0) + n], in_=p_[:, 0:n]),
                             reads=[p_.r], writes=[u_.r])
            for (t0, n) in TOK:
                uo = uoff(t0)
                pa_, pb_ = pc[(im % 2) * 2], pc[(im % 2) * 2 + 1]
                s_, m_ = sa[im % 2], mo[im % 3]
                im += 1
                for half, pd in ((0, pa_), (1, pb_)):
                    for j in range(3):
                        P.op('pe', lambda e, half=half, j=j, pd=pd, dg_=dg_, u_=u_, uo=uo, n=n: e.matmul(
                            pd[:, 0:n], lhsT=dg_[:, half * 3 + j, :], rhs=u_[:, half, uo + j - 1:uo + j - 1 + n],
                            start=(j == 0), stop=(j == 2)), reads=[dg_.r, u_.r], writes=[pd.r])
                P.op('act', lambda e, pa_=pa_, s_=s_, c=c, n=n: e.activation(out=s_[:, 0:n], in_=pa_[:, 0:n], func=AF.Silu,
                                                                            bias=dwb[:, 2 * c:2 * c + 1], scale=1.0),
                     reads=[pa_.r, dwb.r], writes=[s_.r])
                P.op('dve', lambda e, pb_=pb_, s_=s_, m_=m_, c=c, n=n: e.scalar_tensor_tensor(
                    out=m_[:, 0:n], in0=pb_[:, 0:n], scalar=dwb[:, 2 * c + 1:2 * c + 2], in1=s_[:, 0:n], op0=ALU.add, op1=ALU.mult),
                    reads=[pb_.r, dwb.r, s_.r], writes=[m_.r])
                P.dma('sp', lambda e, m_=m_, c=c, t0=t0, n=n: e.dma_start(out=C.MT[:, c, t0:t0 + n], in_=m_[:, 0:n]),
                      reads=[m_.r], writes=[C.rMT])
        P.barrier()
        P.emit()


def phase_ffn_down(C, l):
    P, nc = C.P, C.nc
    mk = C.mk
    with ExitStack() as es:
        wd = mk(es, "dwd", [128, NFC, D], BF16)
        load_w(C, wd, C.f_down[l], 0, D, nk=NFC)
        mt = [mk(es, f"dmt{i}", [128, NFC, 256], BF16) for i in range(2)]
        py = [mk(es, f"dpy{i}", [128, 512], F32, psum=True) for i in range(4)]
        epi = resid_epilogue(C, es, "de", l, 1)
        n = 0
        def dload(tt_):
            m_ = mt[tt_ % 2]
            P.dma('sp', lambda e, m_=m_, t0=tt_ * 256: e.dma_start(out=m_[:], in_=C.MT[:, :, t0:t0 + 256]), reads=[C.rMT], writes=[m_.r])

        dload(0)
        for tt_ in range(T // 256):
            t0 = tt_ * 256
            m_ = mt[tt_ % 2]
            if tt_ + 1 < T // 256:
                dload(tt_ + 1)
            for sub in range(2):
                py0, py1 = py[(n % 2) * 2], py[(n % 2) * 2 + 1]
                n += 1
                for half, pyh in ((0, py0), (1, py1)):
                    for k in range(NFC):
                        P.op('pe', lambda e, k=k, half=half, pyh=pyh, m_=m_, sub=sub: e.matmul(
                            pyh[:], lhsT=m_[:, k, sub * 128:(sub + 1) * 128], rhs=wd[:, k, half * 512:(half + 1) * 512],
                            start=(k == 0), stop=(k == NFC - 1)), reads=[m_.r, wd.r], writes=[pyh.r])
                epi(tt_ * 2 + sub, py0, py1)
        P.barrier()
        P.emit()


def _rope_tables():
    half = 32
    inv = (10000.0 ** (-(np.arange(0, half, 2, dtype=np.float32)) / half)).astype(np.float32)
    tpos = np.arange(4096)
    row = (tpos // 64).astype(np.float32)
    col = (tpos % 64).astype(np.float32)
    ang = np.concatenate([row[:, None] * inv, col[:, None] * inv], axis=-1).astype(np.float32)
    cos = np.concatenate([np.ones((256, 32), np.float32), np.cos(ang).astype(np.float32)], 0)
    sin = np.concatenate([np.zeros((256, 32), np.float32), np.sin(ang).astype(np.float32)], 0)
    tab = np.stack([cos, sin], 0).reshape(2, NT, 128, 32).transpose(0, 2, 1, 3)
    return np.ascontiguousarray(tab)


def _const_tables():
    p = np.arange(128, dtype=np.float32)
    cols = np.stack([p + 1, 128 - p, 127 - p, p], 1)
    j = p[:, None]
    i = p[None, :]
    dpos = np.maximum(i - j, 0)
    dneg = np.maximum(j - i, 0)
    mpos = (i >= j).astype(np.float32)
    mneg = (j >= i).astype(np.float32)
    ret = np.concatenate([cols, dpos, dneg, mpos, mneg], 1).astype(np.float32)
    mlo = (i <= j).astype(np.float32)
    mhi = (j <= i).astype(np.float32)
    mask = np.concatenate([np.tile(mlo, (1, 4)), np.tile(mhi, (1, 4))], 1).astype(np.float32)
    return ret, mask


def _prep_shared(inp):
    f = lambda a: np.ascontiguousarray(np.asarray(a, dtype=np.float32))
    w_in = np.asarray(inp["w_in"], dtype=np.float32)
    perm = []
    for c in range(4):
        perm += list(range(c * 128, (c + 1) * 128)) + list(range(512 + c * 128, 512 + (c + 1) * 128))
    hp = [0, 4, 1, 5, 2, 6, 3, 7]
    o = 1024
    for h in hp:
        perm += list(range(o + h * 64, o + (h + 1) * 64))
    perm += list(range(o + 512, o + 768))
    o = 1024 + 768
    perm += list(range(o, o + 1536))
    o = 1024 + 768 + 1536
    for h in hp:
        perm += list(range(o + h * 64, o + (h + 1) * 64))
    perm += list(range(o + 512, o + 768))
    perm += list(range(4096, 8192))
    perm = np.asarray(perm)
    assert perm.shape[0] == 8192 and np.unique(perm).shape[0] == 8192
    fperm = []
    for c in range(NFC):
        fperm += list(range(c * 128, (c + 1) * 128)) + list(range(DFF + c * 128, DFF + (c + 1) * 128))
    fperm = np.asarray(fperm)
    ret, mask = _const_tables()
    a_vec = np.stack([np.asarray(inp[k], np.float32).reshape(DEPTH, 4, 128) for k in ("a_dw_b", "a_ln_g", "a_ln_b")], 1)
    sh = {
        "ada_w": f(inp["ada_w"]), "ada_b": f(inp["ada_b"]), "norm_g": f(inp["norm_g"]),
        "w_in": f(w_in[:, :, perm]),
        "a_dw": f(np.asarray(inp["a_dw"], np.float32).reshape(DEPTH, 31, 4, 128).transpose(0, 3, 2, 1)),
        "a_vec": f(a_vec.transpose(0, 3, 1, 2)),
        "b_sink": f(inp["b_sink"]), "c_decay": f(np.asarray(inp["c_decay_logit"], np.float32).reshape(DEPTH, 8)),
        "c_gn_g": f(inp["c_gn_g"]),
        "d_qk_g": f(np.stack([np.asarray(inp["d_qn_g"], np.float32), np.asarray(inp["d_kn_g"], np.float32)], 1)),
        "w_br": f(inp["w_br"]), "w_o": f(inp["w_o"]),
        "f_up": f(np.asarray(inp["f_up"], np.float32)[:, :, fperm]),
        "f_dw": f(np.asarray(inp["f_dw"], np.float32)[:, :, fperm].reshape(DEPTH, 3, 2 * NFC, 128).transpose(0, 3, 2, 1)),
        "f_dw_b": f(np.asarray(inp["f_dw_b"], np.float32)[:, fperm].reshape(DEPTH, 2 * NFC, 128).transpose(0, 2, 1)),
        "f_down": f(inp["f_down"]),
        "cst_rope": _rope_tables(), "cst_ret": ret, "cst_mask": mask, "cst_ident": np.eye(128, dtype=np.float32),
    }
    return sh


def _prep_core(inp, b):
    x = np.asarray(inp["x"], np.float32)
    ctx = np.asarray(inp["ctx"], np.float32)
    c = np.asarray(inp["c"], np.float32)
    c_ctx = np.asarray(inp["c_ctx"], np.float32)
    xin = np.ascontiguousarray(np.concatenate([ctx[b], x[b]], 0))
    ccv = np.stack([c[b], c_ctx], 1).reshape(8, 128, 2).transpose(1, 0, 2)
    return {"xin": xin, "cc": np.ascontiguousarray(ccv)}


def kernel(**inputs):
    nc = build_program()
    sh = _prep_shared(inputs)
    in_maps = []
    for b in range(8):
        m = dict(sh)
        m.update(_prep_core(inputs, b))
        in_maps.append(m)
    res = run_bass_kernel_spmd(nc, in_maps, core_ids=list(range(8)))
    return np.stack([np.asarray(r["y"], dtype=np.float32) for r in res.results], 0)
```

```python
import numpy as np
from contextlib import ExitStack
import concourse.bass as bass
import concourse.mybir as mybir
from concourse.alu_op_type import AluOpType as ALU
from concourse.bass_utils import run_bass_kernel_spmd

AF = mybir.ActivationFunctionType
F32 = mybir.dt.float32
BF16 = mybir.dt.bfloat16
AX = mybir.AxisListType

ENGS = ['sp', 'pool', 'act', 'dve', 'pe']

T = 4352
NT = 34
D = 1024
DFF = 2816
NFC = 22
EPS = 1e-6
DEPTH = 4


class Res:
    __slots__ = ('name', 'w', 'r', 'excl')

    def __init__(self, name='', excl=False):
        self.name = name
        self.w = None
        self.r = {}
        self.excl = excl


class Prog:
    def __init__(self, nc, n_dma_sems=32):
        self.nc = nc
        self.n_dma = n_dma_sems
        self.dma_sems = [nc.alloc_semaphore(f"dq{i}") for i in range(n_dma_sems)]
        self.dma_cnt = [0] * n_dma_sems
        self.dma_rr = 0
        self.dma_rr_sw = 0
        self.epoch = 0
        self.ops = {e: [] for e in ENGS}
        self.known = {e: {} for e in ENGS}
        self._new_epoch_sems()
        self.nops = 0

    NSETS = 16

    def _new_epoch_sems(self):
        if not hasattr(self, 'sets'):
            self.sets = []
            self.setcnt = []
        si = self.epoch % self.NSETS
        if si >= len(self.sets):
            self.sets.append({e: self.nc.alloc_semaphore(f"s{si}_{e}") for e in ENGS if e != 'sp'})
            self.setcnt.append({e: 0 for e in ENGS})
        if self.epoch > 0:
            pi = (self.epoch - 1) % self.NSETS
            self.setcnt[pi] = dict(self.ecnt)
        self.esem = self.sets[si]
        self.ecnt = dict(self.setcnt[si])
        self.ecnt0 = dict(self.ecnt)
        for e in ENGS:
            self.known[e] = {k: v for k, v in self.known[e].items() if k[0] == 'd'}

    def _need(self, eng, waits, ev, same_ok):
        if ev is None:
            return
        if ev[0] == 'e':
            _, ep, e, c, sem = ev
            if ep != self.epoch:
                return
            if e == eng and same_ok and eng == 'pe':
                return
            key = ('e', e)
            val = c
        else:
            _, idx, tgt = ev
            key = ('d', idx)
            val = tgt
            sem = self.dma_sems[idx]
        if self.known[eng].get(key, 0) >= val:
            return
        if key not in waits or waits[key][1] < val:
            waits[key] = (sem, val)

    def _hazards(self, eng, reads, writes):
        waits = {}
        for r in reads:
            self._need(eng, waits, r.w, False)
        for w in writes:
            self._need(eng, waits, w.w, True)
            for ev in w.r.values():
                self._need(eng, waits, ev, True)
        for key, (sem, val) in waits.items():
            self.known[eng][key] = val
        return list(waits.values())

    def op(self, eng, fn, reads=(), writes=()):
        if any(r.excl for r in reads):
            writes = list(writes) + [r for r in reads if r.excl and r not in writes]
            reads = [r for r in reads if not r.excl]
        waits = self._hazards(eng, reads, writes)
        self.ecnt[eng] += 1
        sem = self.esem[eng]
        ev = ('e', self.epoch, eng, self.ecnt[eng], sem)
        self.ops[eng].append((waits, fn, sem))
        for r in reads:
            r.r[('e', eng)] = ev
        for w in writes:
            w.w = ev
            w.r = {}
        self.nops += 1

    def dma(self, queue, fns, reads=(), writes=()):
        if not isinstance(fns, (list, tuple)):
            fns = [fns]
        half = self.n_dma // 2
        if queue == 'pool':
            idx = half + self.dma_rr_sw
            self.dma_rr_sw = (self.dma_rr_sw + 1) % (self.n_dma - half)
        else:
            idx = self.dma_rr
            self.dma_rr = (idx + 1) % half
        waits = self._hazards(queue, reads, writes)
        prev = self.dma_cnt[idx]
        if prev > 0 and self.known[queue].get(('d', idx), 0) < prev:
            waits.append((self.dma_sems[idx], prev))
            self.known[queue][('d', idx)] = prev
        self.dma_cnt[idx] += 16 * len(fns)
        ev = ('d', idx, self.dma_cnt[idx])
        self.ops[queue].append((waits, list(fns), self.dma_sems[idx]))
        for r in reads:
            r.r[('d', idx)] = ev
        for w in writes:
            w.w = ev
            w.r = {}
        self.nops += len(fns)

    def barrier(self):
        for eng in ENGS:
            waits = []
            for e2 in ENGS:
                if e2 == 'sp' or self.ecnt[e2] == self.ecnt0[e2]:
                    continue
                if self.known[eng].get(('e', e2), 0) < self.ecnt[e2]:
                    waits.append((self.esem[e2], self.ecnt[e2]))
            for idx in range(self.n_dma):
                if self.dma_cnt[idx] > self.known[eng].get(('d', idx), 0):
                    waits.append((self.dma_sems[idx], self.dma_cnt[idx]))
                    self.known[eng][('d', idx)] = self.dma_cnt[idx]
            self.ops[eng].append((waits, None, None))
        self.epoch += 1
        self._new_epoch_sems()

    def emit(self):
        nc = self.nc
        with nc.Block() as block:
            decos = {'sp': block.sync, 'pool': block.gpsimd, 'act': block.scalar,
                     'dve': block.vector, 'pe': block.tensor}
            for name in ENGS:
                ops = self.ops[name]

                def body(engine, ops=ops):
                    for waits, fn, sem in ops:
                        for (s, val) in waits:
                            engine.wait_ge(s, val)
                        if fn is None:
                            continue
                        if isinstance(fn, list):
                            for f in fn:
                                f(engine).then_inc(sem, 16)
                        else:
                            fn(engine).then_inc(sem, 1)
                decos[name](body)
        self.ops = {e: [] for e in ENGS}


class Buf:
    __slots__ = ('t', 'r')

    def __init__(self, t, name):
        self.t = t
        self.r = Res(name)

    def __getitem__(self, k):
        return self.t[k]


class Ctx:
    pass


def build_program(nlayers=DEPTH, debug=None, phases=None):
    nc = bass.Bass("TRN2", target_bir_lowering=False)
    C = Ctx()
    C.nc = nc
    din = lambda name, shape: nc.dram_tensor(name, shape, F32, kind="ExternalInput").ap()
    xin = din("xin", [T, D])
    cc = din("cc", [128, 8, 2])
    ada_w = din("ada_w", [DEPTH, D, 6 * D])
    ada_b = din("ada_b", [DEPTH, 6 * D])
    norm_g = din("norm_g", [DEPTH, 4, D])
    w_in = din("w_in", [DEPTH, D, 8192])
    a_dw = din("a_dw", [DEPTH, 128, 4, 31])
    a_vec = din("a_vec", [DEPTH, 128, 3, 4])
    b_sink = din("b_sink", [DEPTH, 8])
    c_decay = din("c_decay", [DEPTH, 8])
    c_gn_g = din("c_gn_g", [DEPTH, 512])
    d_qk_g = din("d_qk_g", [DEPTH, 2, 64])
    w_br = din("w_br", [DEPTH, 4, 512, D])
    w_o = din("w_o", [DEPTH, D, D])
    f_up = din("f_up", [DEPTH, D, 2 * DFF])
    f_dw = din("f_dw", [DEPTH, 128, 2 * NFC, 3])
    f_dw_b = din("f_dw_b", [DEPTH, 128, 2 * NFC])
    f_down = din("f_down", [DEPTH, DFF, D])
    cst_rope = din("cst_rope", [2, 128, NT, 32])
    cst_ret = din("cst_ret", [128, 4 + 4 * 128])
    cst_mask = din("cst_mask", [128, 2 * 512])
    cst_ident = din("cst_ident", [128, 128])
    yout = nc.dram_tensor("y", [T - 256, D], F32, kind="ExternalOutput").ap()
    dbg = None
    if debug is not None:
        dbg = nc.dram_tensor("dbg", list(debug[1]), debug[2], kind="ExternalOutput").ap()
    X = nc.dram_tensor("X", [T, D], F32).ap()
    MOD = nc.dram_tensor("MOD", [DEPTH, 2, 6 * D], F32).ap()
    BR = nc.dram_tensor("BR", [4, 128, 4, T], BF16).ap()
    HTd = nc.dram_tensor("HTd", [128, 8, T], BF16).ap()
    MT = nc.dram_tensor("MT", [128, NFC, T], BF16).ap()
    rX, rMOD, rBR, rHTd, rMT = Res('X'), Res('MOD'), Res('BR'), Res('HTd'), Res('MT')

    P = Prog(nc)
    C.P = P

    with ExitStack() as top:
        uid = [0]

        def mk(es, name, shape, dt, psum=False):
            uid[0] += 1
            name = f"{name}_{uid[0]}"
            t = es.enter_context((nc.psum_tensor if psum else nc.sbuf_tensor)(name, shape, dt))
            b = Buf(t, name)
            b.r.excl = psum
            return b

        identf = mk(top, "identf", [128, 128], F32)
        identb = mk(top, "identb", [128, 128], BF16)
        cosT = mk(top, "cosT", [128, NT, 32], F32)
        sinT = mk(top, "sinT", [128, NT, 32], F32)
        retc = mk(top, "retc", [128, 4 + 4 * 128], F32)
        maskw = mk(top, "maskw", [128, 2, 512], BF16)
        onesm = mk(top, "onesm", [128, 128], F32)
        onesr = mk(top, "onesr", [128, 64], F32)

        P.dma('sp', lambda e: e.dma_start(out=identf[:], in_=cst_ident), writes=[identf.r])
        P.op('dve', lambda e: e.tensor_copy(out=identb[:], in_=identf[:]), reads=[identf.r], writes=[identb.r])
        P.dma('sp', [lambda e: e.dma_start(out=cosT[:], in_=cst_rope[0]),
                     lambda e: e.dma_start(out=sinT[:], in_=cst_rope[1])], writes=[cosT.r, sinT.r])
        P.dma('sp', lambda e: e.dma_start(out=retc[:], in_=cst_ret), writes=[retc.r])
        P.dma('pool', lambda e: e.dma_start(out=maskw[:].rearrange("p a n -> p (a n)"), in_=cst_mask), writes=[maskw.r])
        P.op('dve', lambda e: e.memset(onesm[:], 1.0 / 512), writes=[onesm.r])
        P.op('dve', lambda e: e.memset(onesr[:], 1.0), writes=[onesr.r])
        P.dma('sp', [lambda e, i=i: e.dma_start(out=X[i * 1088:(i + 1) * 1088, :], in_=xin[i * 1088:(i + 1) * 1088, :])
                     for i in range(4)], writes=[rX])

        with ExitStack() as es:
            scc = mk(es, "scc", [128, 8, 2], F32)
            wa = [mk(es, f"wa{i}", [128, 8, 512], F32) for i in range(4)]
            ab = [mk(es, f"ab{i}", [2, 512], F32) for i in range(4)]
            ao = [mk(es, f"ao{i}", [2, 512], F32) for i in range(4)]
            pa = [mk(es, f"pa{i}", [128, 512], F32, psum=True) for i in range(4)]
            P.dma('sp', lambda e: e.dma_start(out=scc[:], in_=cc), writes=[scc.r])
            P.op('act', lambda e: e.activation(out=scc[:], in_=scc[:], func=AF.Silu), reads=[scc.r], writes=[scc.r])
            it = 0
            for l in range(nlayers):
                awv = ada_w[l].rearrange("(k p) n -> p k n", p=128)
                for n in range(12):
                    s = it % 4
                    it += 1
                    P.dma('sp', lambda e, s=s, n=n, awv=awv: e.dma_start(out=wa[s][:], in_=awv[:, :, n * 512:(n + 1) * 512]),
                          writes=[wa[s].r])
                    P.dma('sp', lambda e, s=s, n=n, l=l: e.dma_start(
                        out=ab[s][:], in_=ada_b[l, n * 512:(n + 1) * 512].partition_broadcast(2)), writes=[ab[s].r])
                    for k in range(8):
                        P.op('pe', lambda e, s=s, k=k: e.matmul(pa[s][0:2, :], lhsT=scc[:, k, :], rhs=wa[s][:, k, :],
                                                               start=(k == 0), stop=(k == 7)),
                             reads=[scc.r, wa[s].r], writes=[pa[s].r])
                    P.op('dve', lambda e, s=s: e.tensor_tensor(out=ao[s][:], in0=pa[s][0:2, :], in1=ab[s][:], op=ALU.add),
                         reads=[pa[s].r, ab[s].r], writes=[ao[s].r])
                    P.dma('pool', lambda e, s=s, n=n, l=l: e.dma_start(out=MOD[l, :, n * 512:(n + 1) * 512], in_=ao[s][:]),
                          reads=[ao[s].r], writes=[rMOD])
            P.barrier()
            P.emit()

        C.__dict__.update(dict(mk=mk, identf=identf, identb=identb, cosT=cosT, sinT=sinT, retc=retc, maskw=maskw,
                               onesm=onesm, onesr=onesr, X=X, MOD=MOD, BR=BR, HTd=HTd, MT=MT,
                               rX=rX, rMOD=rMOD, rBR=rBR, rHTd=rHTd, rMT=rMT, norm_g=norm_g, w_in=w_in,
                               a_dw=a_dw, a_vec=a_vec, b_sink=b_sink, c_decay=c_decay, c_gn_g=c_gn_g,
                               d_qk_g=d_qk_g, w_br=w_br, w_o=w_o, f_up=f_up, f_dw=f_dw, f_dw_b=f_dw_b,
                               f_down=f_down))

        for l in range(nlayers):
            def want(ph):
                return phases is None or ph in phases
            with ExitStack() as es:
                hT = mk(es, "hT", [128, 8, T], BF16)
                if want('norm1'):
                    phase_norm(C, l, 0, hT, spill=True)
                if want('conv'):
                    phase_conv(C, l, hT)
                if want('win'):
                    phase_attn(C, l, hT, glb=False)
                if want('ret'):
                    phase_ret(C, l, hT)
                if want('glb'):
                    phase_attn(C, l, hT, glb=True)
            if want('merge'):
                phase_merge(C, l)
            with ExitStack() as es_w:
                wd = mk(es_w, "dwd", [128, NFC, D], BF16)
                if want('ffn_down'):
                    load_w(C, wd, f_down[l], 0, D, nk=NFC)
                with ExitStack() as es:
                    hT = mk(es, "hT2", [128, 8, T], BF16)
                    if want('norm2'):
                        phase_norm(C, l, 1, hT, spill=False)
                    if want('ffn_up'):
                        phase_ffn_up(C, l, hT)
                if want('ffn_down'):
                    phase_ffn_down(C, l, wd)

        P.dma('sp', [lambda e, i=i: e.dma_start(out=yout[i * 1024:(i + 1) * 1024, :], in_=X[256 + i * 1024:256 + (i + 1) * 1024, :])
                     for i in range(4)], reads=[rX], writes=[Res('y')])
        if debug is not None:
            src = {'BR': BR, 'X': X, 'MOD': MOD, 'HTd': HTd, 'MT': MT, 'BR0': BR[0], 'BR1': BR[1], 'BR2': BR[2], 'BR3': BR[3]}[debug[0]]
            P.dma('sp', lambda e: e.dma_start(out=dbg, in_=src), reads=[rBR, rX, rMOD, rHTd, rMT], writes=[Res('dbg')])
        P.barrier()
        P.emit()
    return nc


def load_mod_tiles(C, es, l, sub, which):
    P, nc = C.P, C.nc
    out = {}
    base = 3 * sub
    tmp = [C.mk(es, f"mtmp{i}", [128, D], F32) for i in range(2)]
    for r in range(2):
        if which == 'AB':
            A = C.mk(es, f"modA{r}", [128, D], F32)
            B = C.mk(es, f"modB{r}", [128, D], F32)
            P.dma('sp', [lambda e, r=r: e.dma_start(out=tmp[0][:], in_=C.MOD[l, r, (base + 1) * D:(base + 2) * D].partition_broadcast(128)),
                         lambda e: e.dma_start(out=tmp[1][:], in_=C.norm_g[l, 2 * sub].partition_broadcast(128))],
                  reads=[C.rMOD], writes=[tmp[0].r, tmp[1].r])
            P.op('dve', lambda e, A=A: e.scalar_tensor_tensor(out=A[:], in0=tmp[0][:], scalar=1.0, in1=tmp[1][:],
                                                             op0=ALU.add, op1=ALU.mult),
                 reads=[tmp[0].r, tmp[1].r], writes=[A.r])
            P.dma('sp', lambda e, r=r, B=B: e.dma_start(out=B[:], in_=C.MOD[l, r, base * D:(base + 1) * D].partition_broadcast(128)),
                  reads=[C.rMOD], writes=[B.r])
            out[r] = (A, B)
        else:
            G = C.mk(es, f"modG{r}", [128, D], F32)
            P.dma('sp', [lambda e, r=r: e.dma_start(out=tmp[0][:], in_=C.MOD[l, r, (base + 2) * D:(base + 3) * D].partition_broadcast(128)),
                         lambda e: e.dma_start(out=tmp[1][:], in_=C.norm_g[l, 2 * sub + 1].partition_broadcast(128))],
                  reads=[C.rMOD], writes=[tmp[0].r, tmp[1].r])
            P.op('dve', lambda e, G=G: e.tensor_tensor(out=G[:], in0=tmp[0][:], in1=tmp[1][:], op=ALU.mult),
                 reads=[tmp[0].r, tmp[1].r], writes=[G.r])
            out[r] = (G,)
    return out


def phase_norm(C, l, sub, hT, spill):
    P, nc = C.P, C.nc
    with ExitStack() as es:
        mod = load_mod_tiles(C, es, l, sub, 'AB')
        xt = [C.mk(es, f"nxt{i}", [128, D], F32) for i in range(3)]
        sq = C.mk(es, "nsq", [128, D], BF16)
        ss = [C.mk(es, f"nss{i}", [128, 2], F32) for i in range(2)]
        tmp = [C.mk(es, f"ntmp{i}", [128, D], F32) for i in range(2)]
        hb = [C.mk(es, f"nhb{i}", [128, D], BF16) for i in range(2)]
        pT = [C.mk(es, f"npT{i}", [128, D], BF16, psum=True) for i in range(2)]
        def ntile(t):
            r = 1 if t < 2 else 0
            x_, s_, t_, h_, p_ = xt[t % 3], ss[t % 2], tmp[t % 2], hb[t % 2], pT[t % 2]
            P.dma('sp', lambda e, t=t, x_=x_: e.dma_start(out=x_[:], in_=C.X[t * 128:(t + 1) * 128, :]),
                  reads=[C.rX], writes=[x_.r])
            P.op('act', lambda e, x_=x_, s_=s_: e.activation(out=sq[:], in_=x_[:], func=AF.Square, scale=1.0 / 32,
                                                             accum_out=s_[:, 0:1]), reads=[x_.r], writes=[sq.r, s_.r])
            P.op('act', lambda e, s_=s_: e.activation(out=s_[:, 1:2], in_=s_[:, 0:1], func=AF.Sqrt, bias=EPS, scale=1.0),
                 reads=[s_.r], writes=[s_.r])
            P.op('dve', lambda e, s_=s_: e.reciprocal(out=s_[:, 1:2], in_=s_[:, 1:2]), reads=[s_.r], writes=[s_.r])
            A_, B_ = mod[r]
            P.op('dve', lambda e, x_=x_, s_=s_, t_=t_, A_=A_: e.scalar_tensor_tensor(
                out=t_[:], in0=x_[:], scalar=s_[:, 1:2], in1=A_[:], op0=ALU.mult, op1=ALU.mult),
                reads=[x_.r, s_.r, A_.r], writes=[t_.r])
            P.op('dve', lambda e, t_=t_, h_=h_, B_=B_: e.tensor_tensor(out=h_[:], in0=t_[:], in1=B_[:], op=ALU.add),
                 reads=[t_.r, B_.r], writes=[h_.r])
            yield
            for k in range(8):
                P.op('pe', lambda e, k=k, h_=h_, p_=p_: e.transpose(out=p_[:, k * 128:(k + 1) * 128], in_=h_[:, k * 128:(k + 1) * 128],
                                                                    identity=C.identb[:]),
                     reads=[h_.r, C.identb.r], writes=[p_.r])
            P.op('act', lambda e, t=t, p_=p_: e.activation(out=hT[:, :, t * 128:(t + 1) * 128],
                                                           in_=p_[:].rearrange("p (k n) -> p k n", k=8), func=AF.Copy),
                 reads=[p_.r], writes=[hT.r])
        gens = {}
        for t in range(NT + 1):
            if t < NT:
                gens[t] = ntile(t)
                next(gens[t])
            if t >= 1:
                next(gens.pop(t - 1), None)
        if spill:
            P.dma('sp', [lambda e, k=k: e.dma_start(out=C.HTd[:, k, :], in_=hT[:, k, :]) for k in range(8)],
                  reads=[hT.r], writes=[C.rHTd])
        P.barrier()
        P.emit()


def load_w(C, dst, src2d, c0, ncols, nk=8, d0=0):
    v = src2d.rearrange("(k p) n -> p k n", p=128)
    fns = []
    step = 2048
    for a in range(0, ncols, step):
        b = min(ncols, a + step)
        fns.append(lambda e, a=a, b=b: e.dma_start(out=dst[:, :, d0 + a:d0 + b], in_=v[:, :, c0 + a:c0 + b]))
    C.P.dma('pool', fns, writes=[dst.r])


def phase_conv(C, l, hT):
    P, nc = C.P, C.nc
    TOK = [(0, 256)] + [(256 + 512 * i, 512) for i in range(8)]
    with ExitStack() as es:
        mk = C.mk
        zb = mk(es, "czb", [128, 4, T + 60], BF16)
        wv = [mk(es, f"cw{i}", [128, 8, 256], BF16) for i in range(2)]
        dwf = mk(es, "cdwf", [128, 4, 31], F32)
        avec = mk(es, "cavec", [128, 3, 4], F32)
        diag = mk(es, "cdiag", [128, 4 * 31, 128], BF16)
        sg = [mk(es, f"csg{i}", [128, 512], F32) for i in range(2)]
        ycv2 = [mk(es, f"cycv{i}", [128, 4, 256], F32) for i in range(2)]
        ysq = mk(es, "cysq", [128, 4, 256], F32)
        mean2 = [mk(es, f"cmean{i}", [128, 256], F32) for i in range(2)]
        var2 = [mk(es, f"cvar{i}", [128, 256], F32) for i in range(2)]
        tt = [mk(es, f"ctt{i}", [128, 256], F32) for i in range(2)]
        yo = [mk(es, f"cyo{i}", [128, 4, 256], BF16) for i in range(2)]
        pp = [mk(es, f"cpp{i}", [128, 512], F32, psum=True) for i in range(8)]

        def zoff(t0):
            return t0 + 15 if t0 < 256 else t0 + 45

        P.op('dve', lambda e: e.memset(zb[:], 0.0), writes=[zb.r])
        P.dma('sp', [lambda e: e.dma_start(out=dwf[:], in_=C.a_dw[l]),
                     lambda e: e.dma_start(out=avec[:], in_=C.a_vec[l])], writes=[dwf.r, avec.r])
        for c in range(4):
            for j in range(31):
                P.op('dve', lambda e, c=c, j=j: e.tensor_scalar(out=diag[:, c * 31 + j, :], in0=C.identf[:], scalar1=dwf[:, c, j:j + 1],
                                                               scalar2=None, op0=ALU.mult),
                     reads=[C.identf.r, dwf.r], writes=[diag.r])
        it = 0
        for c in range(4):
            w_ = wv[c % 2]
            load_w(C, w_, C.w_in[l], c * 256, 256)
            for (t0, n) in TOK:
                pa_, pg_ = pp[(it % 2) * 2], pp[(it % 2) * 2 + 1]
                s_ = sg[it % 2]
                it += 1
                for half, pd in ((0, pa_), (1, pg_)):
                    for k in range(8):
                        P.op('pe', lambda e, k=k, pd=pd, half=half, w_=w_, t0=t0, n=n: e.matmul(
                            pd[:, 0:n], lhsT=w_[:, k, half * 128:(half + 1) * 128], rhs=hT[:, k, t0:t0 + n],
                            start=(k == 0), stop=(k == 7)), reads=[w_.r, hT.r], writes=[pd.r])
                P.op('act', lambda e, pg_=pg_, s_=s_, n=n: e.activation(out=s_[:, 0:n], in_=pg_[:, 0:n], func=AF.Sigmoid),
                     reads=[pg_.r], writes=[s_.r])
                P.op('dve', lambda e, pa_=pa_, s_=s_, n=n, c=c, t0=t0: e.tensor_tensor(
                    out=zb[:, c, zoff(t0):zoff(t0) + n], in0=pa_[:, 0:n], in1=s_[:, 0:n], op=ALU.mult),
                    reads=[pa_.r, s_.r], writes=[zb.r])
        def ctile(ti):
            t0, n = 256 * ti, 256
            ycv, mean, var = ycv2[ti % 2], mean2[ti % 2], var2[ti % 2]
            zo = zoff(t0)
            for c in range(4):
                pc = pp[c]
                for j in range(31):
                    P.op('pe', lambda e, c=c, j=j, pc=pc, zo=zo, n=n: e.matmul(
                        pc[:, 0:n], lhsT=diag[:, c * 31 + j, :], rhs=zb[:, c, zo + j - 15:zo + j - 15 + n],
                        start=(j == 0), stop=(j == 30)), reads=[diag.r, zb.r], writes=[pc.r])
                P.op('act', lambda e, c=c, pc=pc, n=n: e.activation(out=ycv[:, c, 0:n], in_=pc[:, 0:n], func=AF.Identity,
                                                                   bias=avec[:, 0, c:c + 1], scale=1.0),
                     reads=[pc.r, avec.r], writes=[ycv.r])
            P.op('act', lambda e, n=n: e.activation(out=ysq[:, :, 0:n], in_=ycv[:, :, 0:n], func=AF.Square),
                 reads=[ycv.r], writes=[ysq.r])
            pm, pq = pp[4], pp[5]
            for c in range(4):
                P.op('pe', lambda e, c=c, n=n: e.matmul(pm[:, 0:n], lhsT=C.onesm[:], rhs=ycv[:, c, 0:n], start=(c == 0), stop=(c == 3)),
                     reads=[C.onesm.r, ycv.r], writes=[pm.r])
            for c in range(4):
                P.op('pe', lambda e, c=c, n=n: e.matmul(pq[:, 0:n], lhsT=C.onesm[:], rhs=ysq[:, c, 0:n], start=(c == 0), stop=(c == 3)),
                     reads=[C.onesm.r, ysq.r], writes=[pq.r])
            P.op('act', lambda e, n=n: e.activation(out=mean[:, 0:n], in_=pm[:, 0:n], func=AF.Copy), reads=[pm.r], writes=[mean.r])
            P.op('dve', lambda e, n=n: e.tensor_tensor(out=var[:, 0:n], in0=mean[:, 0:n], in1=mean[:, 0:n], op=ALU.mult),
                 reads=[mean.r], writes=[var.r])
            P.op('dve', lambda e, n=n: e.tensor_tensor(out=var[:, 0:n], in0=pq[:, 0:n], in1=var[:, 0:n], op=ALU.subtract),
                 reads=[pq.r, var.r], writes=[var.r])
            P.op('act', lambda e, n=n: e.activation(out=var[:, 0:n], in_=var[:, 0:n], func=AF.Sqrt, bias=EPS, scale=1.0),
                 reads=[var.r], writes=[var.r])
            P.op('dve', lambda e, n=n: e.reciprocal(out=var[:, 0:n], in_=var[:, 0:n]), reads=[var.r], writes=[var.r])
            yield
            y_ = yo[ti % 2]
            for c in range(4):
                t_ = tt[c % 2]
                P.op('dve', lambda e, c=c, t_=t_, n=n: e.tensor_tensor(out=t_[:, 0:n], in0=ycv[:, c, 0:n], in1=mean[:, 0:n], op=ALU.subtract),
                     reads=[ycv.r, mean.r], writes=[t_.r])
                P.op('dve', lambda e, t_=t_, n=n: e.tensor_tensor(out=t_[:, 0:n], in0=t_[:, 0:n], in1=var[:, 0:n], op=ALU.mult),
                     reads=[t_.r, var.r], writes=[t_.r])
                P.op('act', lambda e, c=c, t_=t_, y_=y_, n=n: e.activation(out=y_[:, c, 0:n], in_=t_[:, 0:n], func=AF.Silu,
                                                                          scale=avec[:, 1, c:c + 1], bias=avec[:, 2, c:c + 1]),
                     reads=[t_.r, avec.r], writes=[y_.r])
            P.dma('sp', lambda e, y_=y_, t0=t0, n=n: e.dma_start(out=C.BR[0, :, :, t0:t0 + n], in_=y_[:, :, 0:n]),
                  reads=[y_.r], writes=[C.rBR])
        gens = {}
        NTC = T // 256
        for ti in range(NTC + 1):
            if ti < NTC:
                gens[ti] = ctile(ti)
                next(gens[ti])
            if ti >= 1:
                next(gens.pop(ti - 1), None)
        P.barrier()
        P.emit()


def phase_attn(C, l, hT, glb):
    P, nc = C.P, C.nc
    mk = C.mk
    col0 = 4096 + 512 - 768 + 0
    col0 = 1024 + (768 + 1536 if glb else 0)
    br = 3 if glb else 1
    with ExitStack() as es:
        wq = mk(es, "awq", [128, 8, 512], BF16)
        wkv = mk(es, "awkv", [128, 8, 256], BF16)
        QT = mk(es, "aQT", [128, 4, T], BF16)
        KT = mk(es, "aKT", [128, 2, T], BF16)
        V = mk(es, "aV", [128, NT, 2, 128], BF16)
        gq = mk(es, "agq", [128, 2, 64], F32)
        esink = mk(es, "aesink", [128, 8], F32)
        load_w(C, wq, C.w_in[l], col0, 512)
        load_w(C, wkv, C.w_in[l], col0 + 512, 256)
        P.op('dve', lambda e: e.memset(V[:], 0.0), writes=[V.r])
        P.op('dve', lambda e: e.memset(V[:, :, :, 64:65], 1.0), writes=[V.r])
        P.op('dve', lambda e: e.memset(KT[:], 0.0), writes=[KT.r])
        if glb:
            P.dma('sp', lambda e: e.dma_start(out=gq[:].rearrange("p a d -> p (a d)"),
                                              in_=C.d_qk_g[l].rearrange("a d -> (a d)").partition_broadcast(128)), writes=[gq.r])
        else:
            P.dma('sp', lambda e: e.dma_start(out=esink[64:65, :], in_=C.b_sink[l:l + 1, :]), writes=[esink.r])
            P.op('act', lambda e: e.activation(out=esink[64:65, :], in_=esink[64:65, :], func=AF.Exp), reads=[esink.r], writes=[esink.r])
        with ExitStack() as es1:
            pq = [mk(es1, f"apq{i}", [128, 512], F32, psum=True) for i in range(2)]
            pk = [mk(es1, f"apk{i}", [128, 512], F32, psum=True) for i in range(2)]
            pT = [mk(es1, f"apT{i}", [128, 1024], BF16, psum=True) for i in range(2)]
            sq = mk(es1, "asq", [128, 640], F32)
            ssum = [mk(es1, f"assum{i}", [128, 10], F32) for i in range(2)]
            qn = [mk(es1, f"aqn{i}", [128, 640], F32) for i in range(2)]
            r1 = [mk(es1, f"ar1{i}", [128, 10, 32], F32) for i in range(2)]
            r2 = [mk(es1, f"ar2{i}", [128, 10, 32], F32) for i in range(2)]
            qb = [mk(es1, f"aqb{i}", [128, 640], BF16) for i in range(2)]
            def tile1(t):
                i2 = t % 2
                pq_, pk_, pT_, ss_, qn_, r1_, r2_, qb_ = pq[i2], pk[i2], pT[i2], ssum[i2], qn[i2], r1[i2], r2[i2], qb[i2]
                for k in range(8):
                    P.op('pe', lambda e, k=k, t=t, pq_=pq_: e.matmul(pq_[:], lhsT=hT[:, k, t * 128:(t + 1) * 128], rhs=wq[:, k, :],
                                                                    start=(k == 0), stop=(k == 7)),
                         reads=[hT.r, wq.r], writes=[pq_.r])
                for k in range(8):
                    P.op('pe', lambda e, k=k, t=t, pk_=pk_: e.matmul(pk_[:, 0:256], lhsT=hT[:, k, t * 128:(t + 1) * 128], rhs=wkv[:, k, :],
                                                                    start=(k == 0), stop=(k == 7)),
                         reads=[hT.r, wkv.r], writes=[pk_.r])
                yield
                P.op('act', lambda e, t=t, pk_=pk_: e.activation(out=V[:, t, :, 0:64], in_=pk_[:, 128:256].rearrange("p (g d) -> p g d", g=2),
                                                                func=AF.Copy), reads=[pk_.r], writes=[V.r])
                if glb:
                    P.op('act', lambda e, pq_=pq_: e.activation(out=sq[:, 0:512], in_=pq_[:], func=AF.Square), reads=[pq_.r], writes=[sq.r])
                    P.op('act', lambda e, pk_=pk_: e.activation(out=sq[:, 512:640], in_=pk_[:, 0:128], func=AF.Square), reads=[pk_.r], writes=[sq.r])
                    P.op('dve', lambda e, ss_=ss_: e.tensor_reduce(out=ss_[:], in_=sq[:].rearrange("p (h d) -> p h d", d=64), axis=AX.X, op=ALU.add),
                         reads=[sq.r], writes=[ss_.r])
                    P.op('act', lambda e, ss_=ss_: e.activation(out=ss_[:], in_=ss_[:], func=AF.Sqrt, bias=EPS, scale=1.0 / 64),
                         reads=[ss_.r], writes=[ss_.r])
                    P.op('dve', lambda e, ss_=ss_: e.reciprocal(out=ss_[:], in_=ss_[:]), reads=[ss_.r], writes=[ss_.r])
                    P.op('dve', lambda e, pq_=pq_, ss_=ss_, qn_=qn_: e.tensor_tensor(
                        out=qn_[:, 0:512].rearrange("p (h d) -> p h d", d=64), in0=pq_[:].rearrange("p (h d) -> p h d", d=64),
                        in1=ss_[:, 0:8].unsqueeze(2).broadcast_to([128, 8, 64]), op=ALU.mult), reads=[pq_.r, ss_.r], writes=[qn_.r])
                    P.op('dve', lambda e, pk_=pk_, ss_=ss_, qn_=qn_: e.tensor_tensor(
                        out=qn_[:, 512:640].rearrange("p (h d) -> p h d", d=64), in0=pk_[:, 0:128].rearrange("p (h d) -> p h d", d=64),
                        in1=ss_[:, 8:10].unsqueeze(2).broadcast_to([128, 2, 64]), op=ALU.mult), reads=[pk_.r, ss_.r], writes=[qn_.r])
                    P.op('dve', lambda e, qn_=qn_: e.tensor_tensor(
                        out=qn_[:, 0:512].rearrange("p (h d) -> p h d", d=64), in0=qn_[:, 0:512].rearrange("p (h d) -> p h d", d=64),
                        in1=gq[:, 0:1, :].broadcast_to([128, 8, 64]), op=ALU.mult), reads=[qn_.r, gq.r], writes=[qn_.r])
                    P.op('dve', lambda e, qn_=qn_: e.tensor_tensor(
                        out=qn_[:, 512:640].rearrange("p (h d) -> p h d", d=64), in0=qn_[:, 512:640].rearrange("p (h d) -> p h d", d=64),
                        in1=gq[:, 1:2, :].broadcast_to([128, 2, 64]), op=ALU.mult), reads=[qn_.r, gq.r], writes=[qn_.r])
                else:
                    P.op('act', lambda e, pq_=pq_, qn_=qn_: e.activation(out=qn_[:, 0:512], in_=pq_[:], func=AF.Copy), reads=[pq_.r], writes=[qn_.r])
                    P.op('act', lambda e, pk_=pk_, qn_=qn_: e.activation(out=qn_[:, 512:640], in_=pk_[:, 0:128], func=AF.Copy), reads=[pk_.r], writes=[qn_.r])
                q3 = qn_[:].rearrange("p (h d) -> p h d", d=64)
                qo3 = qb_[:].rearrange("p (h d) -> p h d", d=64)
                cosb = C.cosT[:, t:t + 1, :].broadcast_to([128, 10, 32])
                sinb = C.sinT[:, t:t + 1, :].broadcast_to([128, 10, 32])
                rd = [qn_.r, C.cosT.r, C.sinT.r]
                P.op('dve', lambda e, q3=q3, r1_=r1_, cosb=cosb: e.tensor_tensor(out=r1_[:], in0=q3[:, :, 0:32], in1=cosb, op=ALU.mult), reads=rd, writes=[r1_.r])
                P.op('dve', lambda e, q3=q3, r2_=r2_, sinb=sinb: e.tensor_tensor(out=r2_[:], in0=q3[:, :, 32:64], in1=sinb, op=ALU.mult), reads=rd, writes=[r2_.r])
                P.op('dve', lambda e, qo3=qo3, r1_=r1_, r2_=r2_: e.tensor_tensor(out=qo3[:, :, 0:32], in0=r1_[:], in1=r2_[:], op=ALU.subtract),
                     reads=[r1_.r, r2_.r], writes=[qb_.r])
                P.op('dve', lambda e, q3=q3, r1_=r1_, cosb=cosb: e.tensor_tensor(out=r1_[:], in0=q3[:, :, 32:64], in1=cosb, op=ALU.mult), reads=rd, writes=[r1_.r])
                P.op('dve', lambda e, q3=q3, r2_=r2_, sinb=sinb: e.tensor_tensor(out=r2_[:], in0=q3[:, :, 0:32], in1=sinb, op=ALU.mult), reads=rd, writes=[r2_.r])
                P.op('dve', lambda e, qo3=qo3, r1_=r1_, r2_=r2_: e.tensor_tensor(out=qo3[:, :, 32:64], in0=r1_[:], in1=r2_[:], op=ALU.add),
                     reads=[r1_.r, r2_.r], writes=[qb_.r])
                for j in range(5):
                    P.op('pe', lambda e, j=j, qb_=qb_, pT_=pT_: e.transpose(out=pT_[:, j * 128:(j + 1) * 128], in_=qb_[:, j * 128:(j + 1) * 128],
                                                                           identity=C.identb[:]),
                         reads=[qb_.r, C.identb.r], writes=[pT_.r])
                P.op('act', lambda e, t=t, pT_=pT_: e.activation(out=QT[:, :, t * 128:(t + 1) * 128],
                                                                in_=pT_[:, 0:512].rearrange("p (j n) -> p j n", j=4), func=AF.Copy),
                     reads=[pT_.r], writes=[QT.r])
                for g in range(2):
                    P.op('act', lambda e, t=t, pT_=pT_, g=g: e.activation(out=KT[g * 64:(g + 1) * 64, g, t * 128:(t + 1) * 128],
                                                                         in_=pT_[g * 64:(g + 1) * 64, 512:640], func=AF.Copy),
                         reads=[pT_.r], writes=[KT.r])
            gens1 = {}
            for t in range(NT + 1):
                if t < NT:
                    gens1[t] = tile1(t)
                    next(gens1[t])
                if t >= 1:
                    next(gens1.pop(t - 1), None)
        P.barrier()
        with ExitStack() as es2:
            NB = 2
            LA = 2
            psS = [mk(es2, f"apsS{i}", [128, NB, 512], F32, psum=True) for i in range(3)]
            psO = [mk(es2, f"apsO{i}", [128, 512], F32, psum=True) for i in range(1)]
            psB = mk(es2, "apsB", [128, 512], F32, psum=True)
            pt = [mk(es2, f"apt{i}", [128, NB, 512], BF16) for i in range(4)]
            den = [mk(es2, f"aden{i}", [128, 512], F32) for i in range(3)]
            osb = [mk(es2, f"aosb{i}", [64, 512], F32) for i in range(3)]
            obf = [mk(es2, f"aobf{i}", [64, 4, 128], BF16) for i in range(3)]
            pending = []
            batches = []
            for qb_i in range(NT):
                if qb_i < 2:
                    kts = [(0, None), (1, None)]
                elif glb:
                    kts = [(k, None) for k in range(NT)]
                else:
                    kts = [(0, None), (1, None)]
                    if qb_i - 1 >= 2:
                        kts.append((qb_i - 1, 0))
                    kts.append((qb_i, None))
                    if qb_i + 1 < NT:
                        kts.append((qb_i + 1, 1))
                for g in range(2):
                    nb = (len(kts) + NB - 1) // NB
                    for bi in range(nb):
                        batches.append((qb_i, g, kts[bi * NB:(bi + 1) * NB], bi == 0, bi == nb - 1))
            BRv = C.BR[br].rearrange("(two d) c t -> d c two t", two=2)
            grp = 0

            def do_pv(i):
                nonlocal grp
                qb_i, g, kts, first, last = batches[i]
                o_ = psO[0]
                p_ = pt[i % 4]
                for j, (kt, m) in enumerate(kts):
                    P.op('pe', lambda e, o_=o_, p_=p_, kt=kt, g=g, j=j, st=(first and j == 0), sp=(last and j == len(kts) - 1): e.matmul(
                        o_[:, :], lhsT=V[:, kt, g, :], rhs=p_[:, j, :], start=st, stop=sp),
                        reads=[V.r, p_.r], writes=[o_.r])
                if last:
                    d_, s_, b_ = den[grp % 3], osb[grp % 3], obf[grp % 3]
                    while pending:
                        pending.pop(0)()
                    if glb:
                        P.op('dve', lambda e, o_=o_, d_=d_: e.tensor_copy(out=d_[64:65, :], in_=o_[64:65, :]), reads=[o_.r], writes=[d_.r])
                    else:
                        P.op('dve', lambda e, o_=o_, d_=d_, g=g: e.tensor_tensor(
                            out=d_[64:65, :].rearrange("p (j n) -> p j n", j=4), in0=o_[64:65, :].rearrange("p (j n) -> p j n", j=4),
                            in1=esink[64:65, 4 * g:4 * g + 4].unsqueeze(2).broadcast_to([1, 4, 128]), op=ALU.add),
                            reads=[o_.r, esink.r], writes=[d_.r])
                        P.op('act', lambda e, d_=d_: e.activation(out=d_[64:65, :], in_=d_[64:65, :], func=AF.Ln), reads=[d_.r], writes=[d_.r])
                        P.op('act', lambda e, d_=d_: e.activation(out=d_[64:65, :], in_=d_[64:65, :], func=AF.Exp, scale=-1.0), reads=[d_.r], writes=[d_.r])
                    P.op('dve', lambda e, o_=o_, s_=s_: e.tensor_copy(out=s_[:], in_=o_[0:64, :]), reads=[o_.r], writes=[s_.r])
                    if glb:
                        P.op('dve', lambda e, d_=d_: e.reciprocal(out=d_[64:65, :], in_=d_[64:65, :]), reads=[d_.r], writes=[d_.r])

                    def fin(d_=d_, s_=s_, b_=b_, g=g, qb_i=qb_i):
                        P.op('pe', lambda e: e.matmul(psB[0:64, :], lhsT=C.onesr[64:65, :], rhs=d_[64:65, :], start=True, stop=True),
                             reads=[C.onesr.r, d_.r], writes=[psB.r])
                        P.op('dve', lambda e: e.tensor_tensor(out=b_[:].rearrange("p j n -> p (j n)"), in0=s_[:], in1=psB[0:64, :], op=ALU.mult),
                             reads=[s_.r, psB.r], writes=[b_.r])
                        P.dma('sp', [lambda e, cl=cl: e.dma_start(
                            out=BRv[:, 2 * g + cl, :, qb_i * 128:(qb_i + 1) * 128],
                            in_=b_[:, 2 * cl:2 * cl + 2, :]) for cl in range(2)], reads=[b_.r], writes=[C.rBR])
                    pending.append(fin)
                    grp += 1

            for i in range(len(batches) + LA):
                if i < len(batches):
                    qb_i, g, kts, first, last = batches[i]
                    s_ = psS[i % 3]
                    p_ = pt[i % 4]
                    n = len(kts)
                    for j, (kt, m) in enumerate(kts):
                        P.op('pe', lambda e, s_=s_, kt=kt, g=g, qb_i=qb_i, j=j: e.matmul(
                            s_[:, j, :].rearrange("p (j n) -> p j n", j=4), lhsT=KT[:, g, kt * 128:(kt + 1) * 128],
                            rhs=QT[:, :, qb_i * 128:(qb_i + 1) * 128], start=True, stop=True),
                            reads=[KT.r, QT.r], writes=[s_.r])
                    P.op('act', lambda e, s_=s_, p_=p_, n=n: e.activation(out=p_[:, 0:n, :], in_=s_[:, 0:n, :], func=AF.Exp, scale=0.125),
                         reads=[s_.r], writes=[p_.r])
                    for j, (kt, m) in enumerate(kts):
                        if m is not None:
                            P.op('dve', lambda e, p_=p_, m=m, j=j: e.tensor_tensor(out=p_[:, j, :], in0=p_[:, j, :], in1=C.maskw[:, m, :], op=ALU.mult),
                                 reads=[p_.r, C.maskw.r], writes=[p_.r])
                if i >= LA:
                    do_pv(i - LA)
            while pending:
                pending.pop(0)()
        P.barrier()
        P.emit()


def phase_ret(C, l, hT):
    for hh in range(2):
        _ret_half(C, l, hT, hh)


def _ret_half(C, l, hT, hh):
    P, nc = C.P, C.nc
    mk = C.mk
    col0 = 1024 + 768
    cq, ck, cv, cg = col0 + hh * 128, col0 + 256 + hh * 128, col0 + 512 + hh * 256, col0 + 1024 + hh * 256
    with ExitStack() as es:
        wqk = mk(es, "rwqk", [128, 8, 256], BF16)
        wv = mk(es, "rwv", [128, 8, 256], BF16)
        wg = mk(es, "rwg", [128, 8, 256], BF16)
        kd = mk(es, "rkd", [128, NT, 256], BF16)
        vb = mk(es, "rvb", [128, NT, 256], BF16)
        Sall = mk(es, "rSall", [128, NT, 256], BF16)
        gsb = mk(es, "rgsb", [128, NT, 256], BF16)
        S = mk(es, "rS", [128, 256], F32)
        lg = mk(es, "rlg", [128, 8], F32)
        dq = mk(es, "rdq", [128, 2, 2], F32)
        dk = mk(es, "rdk", [128, 2, 2], F32)
        cd = mk(es, "rcd", [128, 2, 128], F32)
        Dm = mk(es, "rDm", [128, 2, 128], F32)
        tmpd = mk(es, "rtmpd", [128, 128], F32)
        gng = mk(es, "rgng", [128, 256], F32)
        c128 = mk(es, "rc128", [128, 1], F32)
        load_w(C, wqk, C.w_in[l], cq, 128, d0=0)
        load_w(C, wqk, C.w_in[l], ck, 128, d0=128)
        load_w(C, wv, C.w_in[l], cv, 256)
        load_w(C, wg, C.w_in[l], cg, 256)
        P.dma('sp', [lambda e: e.dma_start(out=lg[:], in_=C.c_decay[l].partition_broadcast(128)),
                     lambda e: e.dma_start(out=gng[:], in_=C.c_gn_g[l, hh * 256:(hh + 1) * 256].partition_broadcast(128))],
              writes=[lg.r, gng.r])
        P.op('act', lambda e: e.activation(out=lg[:], in_=lg[:], func=AF.Exp, scale=-1.0), reads=[lg.r], writes=[lg.r])
        P.op('act', lambda e: e.activation(out=lg[:], in_=lg[:], func=AF.Ln, bias=1.0, scale=1.0), reads=[lg.r], writes=[lg.r])
        P.op('dve', lambda e: e.tensor_scalar(out=lg[:], in0=lg[:], scalar1=-1.0, scalar2=None, op0=ALU.mult), reads=[lg.r], writes=[lg.r])
        rc = C.retc
        P.op('dve', lambda e: e.tensor_tensor(out=c128[:], in0=rc[:, 0:1], in1=rc[:, 2:3], op=ALU.add), reads=[rc.r], writes=[c128.r])
        for h2 in range(2):
            h = 2 * hh + h2
            P.op('act', lambda e, h=h, h2=h2: e.activation(out=dq[:, h2, 0:1], in_=rc[:, 0:1], func=AF.Exp, scale=lg[:, h:h + 1]), reads=[rc.r, lg.r], writes=[dq.r])
            P.op('act', lambda e, h=h, h2=h2: e.activation(out=dq[:, h2, 1:2], in_=rc[:, 1:2], func=AF.Exp, scale=lg[:, 4 + h:5 + h]), reads=[rc.r, lg.r], writes=[dq.r])
            P.op('act', lambda e, h=h, h2=h2: e.activation(out=dk[:, h2, 0:1], in_=rc[:, 2:3], func=AF.Exp, scale=lg[:, h:h + 1]), reads=[rc.r, lg.r], writes=[dk.r])
            P.op('act', lambda e, h=h, h2=h2: e.activation(out=dk[:, h2, 1:2], in_=rc[:, 3:4], func=AF.Exp, scale=lg[:, 4 + h:5 + h]), reads=[rc.r, lg.r], writes=[dk.r])
            for (r0, r1_, off) in ((0, 64, 0), (64, 128, 4)):
                P.op('act', lambda e, h=h, h2=h2, r0=r0, r1_=r1_, off=off: e.activation(
                    out=cd[r0:r1_, h2, 0:1], in_=c128[r0:r1_, :], func=AF.Exp, scale=lg[r0:r1_, off + h:off + h + 1]),
                    reads=[c128.r, lg.r], writes=[cd.r])
            P.op('act', lambda e, h=h: e.activation(out=tmpd[:], in_=rc[:, 4:132], func=AF.Exp, scale=lg[:, h:h + 1]), reads=[rc.r, lg.r], writes=[tmpd.r])
            P.op('dve', lambda e, h2=h2: e.tensor_tensor(out=Dm[:, h2, :], in0=tmpd[:], in1=rc[:, 260:388], op=ALU.mult), reads=[tmpd.r, rc.r], writes=[Dm.r])
            P.op('act', lambda e, h=h: e.activation(out=tmpd[:], in_=rc[:, 132:260], func=AF.Exp, scale=lg[:, 4 + h:5 + h]), reads=[rc.r, lg.r, Dm.r], writes=[tmpd.r])
            P.op('dve', lambda e: e.tensor_tensor(out=tmpd[:], in0=tmpd[:], in1=rc[:, 388:516], op=ALU.mult), reads=[tmpd.r, rc.r], writes=[tmpd.r])
            P.op('dve', lambda e, h2=h2: e.tensor_tensor(out=Dm[:, h2, :], in0=Dm[:, h2, :], in1=tmpd[:], op=ALU.add), reads=[tmpd.r, Dm.r], writes=[Dm.r])
        P.op('dve', lambda e: e.tensor_scalar(out=Dm[:], in0=Dm[:], scalar1=0.125, scalar2=None, op0=ALU.mult), reads=[Dm.r], writes=[Dm.r])
        P.op('dve', lambda e: e.tensor_scalar(out=dk[:], in0=dk[:], scalar1=0.125, scalar2=None, op0=ALU.mult), reads=[dk.r], writes=[dk.r])
        P.op('dve', lambda e: e.tensor_copy(out=cd[:, :, 1:128], in_=cd[:, :, 0:1].broadcast_to([128, 2, 127])), reads=[cd.r], writes=[cd.r])
        P.op('dve', lambda e: e.memset(S[:], 0.0), writes=[S.r])
        cdf = cd[:].rearrange("p h n -> p (h n)")
        import os
        RS = os.environ.get('RET_STOP', '')
        if RS == 'setup':
            P.barrier(); P.emit(); return
        with ExitStack() as es1:
            pk = [mk(es1, f"rpk{i}", [128, 512], F32, psum=True) for i in range(2)]
            pv = [mk(es1, f"rpv{i}", [128, 512], F32, psum=True) for i in range(2)]
            pU = [mk(es1, f"rpU{i}", [128, 512], F32, psum=True) for i in range(2)]
            pG = [mk(es1, f"rpG{i}", [128, 512], F32, psum=True) for i in range(2)]

            def u_mm(t, pU_):
                for h2 in range(2):
                    hs = slice(h2 * 128, (h2 + 1) * 128)
                    P.op('pe', lambda e, hs=hs, t=t, pU_=pU_: e.matmul(pU_[:, hs], lhsT=kd[:, t, hs], rhs=vb[:, t, hs], start=True, stop=True),
                         reads=[kd.r, vb.r], writes=[pU_.r])

            def chain(t, pU_, r0, r1_):
                P.op('dve', lambda e: e.tensor_copy(out=Sall[r0:r1_, t, :], in_=S[r0:r1_, :]), reads=[S.r], writes=[Sall.r])
                P.op('dve', lambda e: e.tensor_tensor(out=S[r0:r1_, :], in0=S[r0:r1_, :], in1=cdf[r0:r1_, :], op=ALU.mult),
                     reads=[S.r, cd.r], writes=[S.r])
                P.op('dve', lambda e: e.tensor_tensor(out=S[r0:r1_, :], in0=S[r0:r1_, :], in1=pU_[r0:r1_, 0:256], op=ALU.add),
                     reads=[S.r, pU_.r], writes=[S.r])

            for t in range(NT):
                i2 = t % 2
                pk_, pv_, pU_ = pk[i2], pv[i2], pU[i2]
                tk = slice(t * 128, (t + 1) * 128)
                for k in range(8):
                    P.op('pe', lambda e, k=k, tk=tk, pk_=pk_: e.matmul(pk_[:, 0:128], lhsT=hT[:, k, tk], rhs=wqk[:, k, 128:256], start=(k == 0), stop=(k == 7)),
                         reads=[hT.r, wqk.r], writes=[pk_.r])
                for k in range(8):
                    P.op('pe', lambda e, k=k, tk=tk, pv_=pv_: e.matmul(pv_[:, 0:256], lhsT=hT[:, k, tk], rhs=wv[:, k, :], start=(k == 0), stop=(k == 7)),
                         reads=[hT.r, wv.r], writes=[pv_.r])
                pG_ = pG[i2]
                for k in range(8):
                    P.op('pe', lambda e, k=k, tk=tk, pG_=pG_: e.matmul(pG_[:, 0:256], lhsT=hT[:, k, tk], rhs=wg[:, k, :], start=(k == 0), stop=(k == 7)),
                         reads=[hT.r, wg.r], writes=[pG_.r])
                P.op('act', lambda e, t=t, pv_=pv_: e.activation(out=vb[:, t, :], in_=pv_[:, 0:256], func=AF.Copy), reads=[pv_.r], writes=[vb.r])
                P.op('act', lambda e, t=t, pG_=pG_: e.activation(out=gsb[:, t, :], in_=pG_[:, 0:256], func=AF.Silu), reads=[pG_.r], writes=[gsb.r])
                P.op('pool', lambda e, t=t: e.tensor_tensor(out=gsb[:, t, :], in0=gsb[:, t, :], in1=gng[:], op=ALU.mult), reads=[gsb.r, gng.r], writes=[gsb.r])
                P.op('dve', lambda e, t=t, pk_=pk_: e.tensor_tensor(
                    out=kd[:, t, :].rearrange("p (h a d) -> p h a d", h=2, a=2),
                    in0=pk_[:, 0:128].rearrange("p (h d) -> p h d", h=2).unsqueeze(2).broadcast_to([128, 2, 2, 64]),
                    in1=dk[:].unsqueeze(3).broadcast_to([128, 2, 2, 64]), op=ALU.mult), reads=[pk_.r, dk.r], writes=[kd.r])
                u_mm(t, pU_)
                chain(t, pU_, 0, 64)
            order = [1, 0] + list(range(NT - 1, 1, -1))
            for i, t in enumerate(order):
                pU_ = pU[i % 2]
                u_mm(t, pU_)
                chain(t, pU_, 64, 128)
        P.barrier()
        if RS == 'pass1':
            P.barrier(); P.emit(); return
        with ExitStack() as es2:
            pqk = mk(es2, "rpqk", [128, 512], F32, psum=True)
            pT = mk(es2, "rpT", [128, 1024], BF16, psum=True)
            pSG = [mk(es2, f"rpSG{i}", [128, 512], F32, psum=True) for i in range(2)]
            pO = [mk(es2, f"rpO{i}", [128, 512], F32, psum=True) for i in range(2)]
            pT2 = mk(es2, "rpT2", [128, 1024], BF16, psum=True)
            qkb = [mk(es2, f"rqkb{i}", [128, 256], BF16) for i in range(2)]
            qdb = [mk(es2, f"rqdb{i}", [128, 256], BF16) for i in range(2)]
            qkT = [mk(es2, f"rqkT{i}", [128, 2, 128], BF16) for i in range(2)]
            qdT = [mk(es2, f"rqdT{i}", [128, 2, 128], BF16) for i in range(2)]
            pTb = [mk(es2, f"rpTb{i}", [128, 256], BF16) for i in range(2)]
            st = [mk(es2, f"rst{i}", [128, 2, 6], F32) for i in range(2)]
            mv = [mk(es2, f"rmv{i}", [128, 2, 2], F32) for i in range(2)]
            on = [mk(es2, f"ron{i}", [128, 256], F32) for i in range(2)]
            gs = [mk(es2, f"rgs{i}", [128, 256], F32) for i in range(2)]
            sqb = mk(es2, "rsqb", [128, 256], F32)
            yb = [mk(es2, f"ryb{i}", [128, 256], BF16) for i in range(2)]
            yT = [mk(es2, f"ryT{i}", [128, 2, 128], BF16) for i in range(2)]
            def chunk(t):
                i2 = t % 2
                pS_, pO_, qkb_, qdb_, qkT_, qdT_, pTb_, st_, mv_, on_, gs_, yb_, yT_ = (
                    pSG[0], pO[i2], qkb[i2], qdb[i2], qkT[i2], qdT[i2], pTb[i2], st[i2], mv[i2], on[i2], gs[i2], yb[i2], yT[i2])
                tk = slice(t * 128, (t + 1) * 128)
                for k in range(8):
                    P.op('pe', lambda e, k=k, tk=tk: e.matmul(pqk[:, 0:256], lhsT=hT[:, k, tk], rhs=wqk[:, k, :], start=(k == 0), stop=(k == 7)),
                         reads=[hT.r, wqk.r], writes=[pqk.r])
                P.op('act', lambda e, qkb_=qkb_: e.activation(out=qkb_[:], in_=pqk[:, 0:256], func=AF.Copy), reads=[pqk.r], writes=[qkb_.r])
                P.op('dve', lambda e, qdb_=qdb_: e.tensor_tensor(
                    out=qdb_[:].rearrange("p (h a d) -> p h a d", h=2, a=2),
                    in0=pqk[:, 0:128].rearrange("p (h d) -> p h d", h=2).unsqueeze(2).broadcast_to([128, 2, 2, 64]),
                    in1=dq[:].unsqueeze(3).broadcast_to([128, 2, 2, 64]), op=ALU.mult), reads=[pqk.r, dq.r], writes=[qdb_.r])
                for j in range(2):
                    P.op('pe', lambda e, j=j, qkb_=qkb_: e.transpose(out=pT[:, j * 128:(j + 1) * 128], in_=qkb_[:, j * 128:(j + 1) * 128],
                                                                    identity=C.identb[:]), reads=[qkb_.r, C.identb.r], writes=[pT.r])
                for j in range(2):
                    P.op('pe', lambda e, j=j, qdb_=qdb_: e.transpose(out=pT[:, 256 + j * 128:256 + (j + 1) * 128], in_=qdb_[:, j * 128:(j + 1) * 128],
                                                                    identity=C.identb[:]), reads=[qdb_.r, C.identb.r], writes=[pT.r])
                P.op('act', lambda e, qkT_=qkT_: e.activation(out=qkT_[:], in_=pT[:, 0:256].rearrange("p (j n) -> p j n", j=2), func=AF.Copy),
                     reads=[pT.r], writes=[qkT_.r])
                P.op('dve', lambda e, qdT_=qdT_: e.tensor_copy(out=qdT_[:], in_=pT[:, 256:512].rearrange("p (j n) -> p j n", j=2)),
                     reads=[pT.r], writes=[qdT_.r])
                for h2 in range(2):
                    b0 = h2 * 64
                    P.op('pe', lambda e, h2=h2, b0=b0, qkT_=qkT_: e.matmul(pSG[h2][:, 0:128], lhsT=qkT_[b0:b0 + 64, 1, :],
                                                                          rhs=qkT_[b0:b0 + 64, 0, :], start=True, stop=True),
                         reads=[qkT_.r], writes=[pSG[h2].r])
                for h2 in range(2):
                    P.op('dve', lambda e, h2=h2, pTb_=pTb_: e.tensor_tensor(out=pTb_[:, h2 * 128:(h2 + 1) * 128], in0=pSG[h2][:, 0:128], in1=Dm[:, h2, :], op=ALU.mult),
                         reads=[pSG[h2].r, Dm.r], writes=[pTb_.r])
                for h2 in range(2):
                    hs = slice(h2 * 128, (h2 + 1) * 128)
                    P.op('pe', lambda e, hs=hs, t=t, pO_=pO_, pTb_=pTb_: e.matmul(pO_[:, hs], lhsT=pTb_[:, hs], rhs=vb[:, t, hs], start=True, stop=False),
                         reads=[pTb_.r, vb.r], writes=[pO_.r])
                    P.op('pe', lambda e, h2=h2, hs=hs, t=t, pO_=pO_, qdT_=qdT_: e.matmul(pO_[:, hs], lhsT=qdT_[:, h2, :], rhs=Sall[:, t, hs], start=False, stop=True),
                         reads=[qdT_.r, Sall.r], writes=[pO_.r])
                yield
                P.op('act', lambda e, pO_=pO_, on_=on_: e.activation(out=on_[:], in_=pO_[:, 0:256], func=AF.Copy), reads=[pO_.r], writes=[on_.r])
                P.op('act', lambda e, pO_=pO_: e.activation(out=sqb[:], in_=pO_[:, 0:256], func=AF.Square), reads=[pO_.r], writes=[sqb.r])
                P.op('dve', lambda e, on_=on_, st_=st_: e.tensor_reduce(out=st_[:, 0, 0:2], in_=on_[:].rearrange("p (h d) -> p h d", h=2), axis=AX.X, op=ALU.add),
                     reads=[on_.r], writes=[st_.r])
                P.op('dve', lambda e, st_=st_: e.tensor_reduce(out=st_[:, 0, 2:4], in_=sqb[:].rearrange("p (h d) -> p h d", h=2), axis=AX.X, op=ALU.add),
                     reads=[sqb.r], writes=[st_.r])
                P.op('dve', lambda e, st_=st_, mv_=mv_: e.tensor_scalar(out=mv_[:, 0, :], in0=st_[:, 0, 0:2], scalar1=1.0 / 128, scalar2=None, op0=ALU.mult),
                     reads=[st_.r], writes=[mv_.r])
                P.op('dve', lambda e, st_=st_, mv_=mv_: e.tensor_tensor(out=st_[:, 1, 0:2], in0=mv_[:, 0, :], in1=mv_[:, 0, :], op=ALU.mult),
                     reads=[mv_.r], writes=[st_.r])
                P.op('dve', lambda e, st_=st_, mv_=mv_: e.scalar_tensor_tensor(out=mv_[:, 1, :], in0=st_[:, 0, 2:4], scalar=1.0 / 128, in1=st_[:, 1, 0:2],
                                                                               op0=ALU.mult, op1=ALU.subtract), reads=[st_.r], writes=[mv_.r])
                P.op('act', lambda e, mv_=mv_: e.activation(out=mv_[:, 1, :], in_=mv_[:, 1, :], func=AF.Sqrt, bias=EPS, scale=1.0), reads=[mv_.r], writes=[mv_.r])
                P.op('dve', lambda e, mv_=mv_: e.reciprocal(out=mv_[:, 1, :], in_=mv_[:, 1, :]), reads=[mv_.r], writes=[mv_.r])
                for h2 in range(2):
                    hs = slice(h2 * 128, (h2 + 1) * 128)
                    P.op('dve', lambda e, h2=h2, hs=hs, on_=on_, mv_=mv_: e.tensor_scalar(
                        out=on_[:, hs], in0=on_[:, hs], scalar1=mv_[:, 0, h2:h2 + 1], scalar2=mv_[:, 1, h2:h2 + 1], op0=ALU.subtract, op1=ALU.mult),
                        reads=[on_.r, mv_.r], writes=[on_.r])
                P.op('dve', lambda e, on_=on_, t=t, yb_=yb_: e.tensor_tensor(out=yb_[:], in0=on_[:], in1=gsb[:, t, :], op=ALU.mult),
                     reads=[on_.r, gsb.r], writes=[yb_.r])
                for j in range(2):
                    P.op('pe', lambda e, j=j, yb_=yb_: e.transpose(out=pT2[:, j * 128:(j + 1) * 128], in_=yb_[:, j * 128:(j + 1) * 128],
                                                                  identity=C.identb[:]), reads=[yb_.r, C.identb.r], writes=[pT2.r])
                P.op('act', lambda e, yT_=yT_: e.activation(out=yT_[:], in_=pT2[:, 0:256].rearrange("p (j n) -> p j n", j=2), func=AF.Copy),
                     reads=[pT2.r], writes=[yT_.r])
                P.dma('sp', lambda e, tk=tk, yT_=yT_: e.dma_start(out=C.BR[2, :, 2 * hh:2 * hh + 2, tk], in_=yT_[:]), reads=[yT_.r], writes=[C.rBR])
            gens = {}
            for t in range(NT + 1):
                if t < NT:
                    gens[t] = chunk(t)
                    next(gens[t])
                if t >= 1:
                    next(gens.pop(t - 1), None)
        P.barrier()
        P.emit()


def resid_epilogue(C, es, name, l, sub):
    mk = C.mk
    st = Ctx()
    st.mod = load_mod_tiles(C, es, l, sub, 'G')
    st.xt = [mk(es, f"{name}xt{i}", [128, D], F32) for i in range(2)]
    st.sq = mk(es, f"{name}sq", [128, 512], BF16)
    st.ss = [mk(es, f"{name}ss{i}", [128, 4], F32) for i in range(2)]
    st.tmp = [mk(es, f"{name}tmp{i}", [128, D], F32) for i in range(2)]
    st.n = 0

    def run(t, py0, py1):
        P = C.P
        i2 = st.n % 2
        st.n += 1
        r = 1 if t < 2 else 0
        x_, s_, t_ = st.xt[i2], st.ss[i2], st.tmp[i2]
        P.dma('sp', lambda e: e.dma_start(out=x_[:], in_=C.X[t * 128:(t + 1) * 128, :]), reads=[C.rX], writes=[x_.r])
        P.op('act', lambda e: e.activation(out=st.sq[:], in_=py0[:], func=AF.Square, scale=1.0 / 32, accum_out=s_[:, 0:1]),
             reads=[py0.r], writes=[st.sq.r, s_.r])
        P.op('act', lambda e: e.activation(out=st.sq[:], in_=py1[:], func=AF.Square, scale=1.0 / 32, accum_out=s_[:, 1:2]),
             reads=[py1.r], writes=[st.sq.r, s_.r])
        P.op('dve', lambda e: e.tensor_tensor(out=s_[:, 2:3], in0=s_[:, 0:1], in1=s_[:, 1:2], op=ALU.add), reads=[s_.r], writes=[s_.r])
        P.op('act', lambda e: e.activation(out=s_[:, 3:4], in_=s_[:, 2:3], func=AF.Sqrt, bias=EPS, scale=1.0), reads=[s_.r], writes=[s_.r])
        P.op('dve', lambda e: e.reciprocal(out=s_[:, 3:4], in_=s_[:, 3:4]), reads=[s_.r], writes=[s_.r])
        G = st.mod[r][0]
        for half, py in ((0, py0), (1, py1)):
            P.op('dve', lambda e, half=half, py=py: e.scalar_tensor_tensor(
                out=t_[:, half * 512:(half + 1) * 512], in0=py[:], scalar=s_[:, 3:4], in1=G[:, half * 512:(half + 1) * 512],
                op0=ALU.mult, op1=ALU.mult), reads=[py.r, s_.r, G.r], writes=[t_.r])
        P.op('pool', lambda e: e.tensor_tensor(out=t_[:], in0=t_[:], in1=x_[:], op=ALU.add), reads=[t_.r, x_.r], writes=[t_.r])
        P.dma('pool', lambda e: e.dma_start(out=C.X[t * 128:(t + 1) * 128, :], in_=t_[:]), reads=[t_.r], writes=[C.rX])
    return run


def phase_merge(C, l):
    P, nc = C.P, C.nc
    mk = C.mk
    with ExitStack() as es:
        wgt = mk(es, "mwg", [128, 8, 4096], BF16)
        wbr = mk(es, "mwbr", [128, 16, D], BF16)
        wo = mk(es, "mwo", [128, 8, D], BF16)
        load_w(C, wgt, C.w_in[l], 4096, 4096)
        load_w(C, wbr, C.w_br[l].rearrange("b c d -> (b c) d"), 0, D, nk=16)
        load_w(C, wo, C.w_o[l], 0, D)
        hTt = [mk(es, f"mhT{i}", [128, 8, 256], BF16) for i in range(2)]
        brt = [mk(es, f"mbr{i}", [128, 4, 4, 256], BF16) for i in range(2)]
        sgm = [mk(es, f"msg{i}", [128, 4, 256], F32) for i in range(2)]
        prod = [mk(es, f"mprod{i}", [128, 4, 256], F32) for i in range(2)]
        accT = [mk(es, f"macc{i}", [128, 8, 256], BF16) for i in range(2)]
        pg = [[mk(es, f"mpg{i}{j}", [128, 512], F32, psum=True) for j in range(2)] for i in range(1)]
        pb = [[mk(es, f"mpb{i}{j}", [128, 512], F32, psum=True) for j in range(2)] for i in range(1)]
        py = [mk(es, f"mpy{i}", [128, 512], F32, psum=True) for i in range(4)]
        epi = resid_epilogue(C, es, "me", l, 0)
        def mload(tt_):
            t0 = tt_ * 256
            h_, b_ = hTt[tt_ % 2], brt[tt_ % 2]
            P.dma('sp', lambda e, h_=h_, t0=t0: e.dma_start(out=h_[:], in_=C.HTd[:, :, t0:t0 + 256]), reads=[C.rHTd], writes=[h_.r])
            P.dma('sp', [lambda e, b_=b_, t0=t0, i=i: e.dma_start(out=b_[:, i], in_=C.BR[i, :, :, t0:t0 + 256]) for i in range(4)],
                  reads=[C.rBR], writes=[b_.r])

        mload(0)
        for tt_ in range(T // 256):
            t0 = tt_ * 256
            i2 = tt_ % 2
            h_, b_, a_ = hTt[i2], brt[i2], accT[i2]
            if tt_ + 1 < T // 256:
                mload(tt_ + 1)
            for dc in range(8):
                j2 = dc % 2
                s_, p_ = sgm[j2], prod[j2]
                ds_ = slice(dc * 128, (dc + 1) * 128)
                for hh in range(2):
                  for i in (2 * hh, 2 * hh + 1):
                    pg_ = pg[0][i // 2]
                    for k in range(8):
                        P.op('pe', lambda e, i=i, k=k, pg_=pg_, h_=h_, dc=dc: e.matmul(
                            pg_[:, (i % 2) * 256:(i % 2 + 1) * 256], lhsT=wgt[:, k, i * 1024 + dc * 128:i * 1024 + (dc + 1) * 128],
                            rhs=h_[:, k, :], start=(k == 0), stop=(k == 7)), reads=[wgt.r, h_.r], writes=[pg_.r])
                  for i in (2 * hh, 2 * hh + 1):
                    pb_ = pb[0][i // 2]
                    for k in range(4):
                        P.op('pe', lambda e, i=i, k=k, pb_=pb_, b_=b_, ds_=ds_: e.matmul(
                            pb_[:, (i % 2) * 256:(i % 2 + 1) * 256], lhsT=wbr[:, i * 4 + k, ds_], rhs=b_[:, i, k, :],
                            start=(k == 0), stop=(k == 3)), reads=[wbr.r, b_.r], writes=[pb_.r])
                for hh in range(2):
                    P.op('act', lambda e, hh=hh, s_=s_: e.activation(out=s_[:, 2 * hh:2 * hh + 2, :].rearrange("p a n -> p (a n)"), in_=pg[0][hh][:],
                                                                    func=AF.Sigmoid), reads=[pg[0][hh].r], writes=[s_.r])
                    P.op('dve', lambda e, hh=hh, s_=s_, p_=p_: e.tensor_tensor(
                        out=p_[:, 2 * hh:2 * hh + 2, :].rearrange("p a n -> p (a n)"), in0=s_[:, 2 * hh:2 * hh + 2, :].rearrange("p a n -> p (a n)"),
                        in1=pb[0][hh][:], op=ALU.mult), reads=[s_.r, pb[0][hh].r], writes=[p_.r])
                P.op('dve', lambda e, p_=p_: e.tensor_tensor(out=p_[:, 0:2, :], in0=p_[:, 0:2, :], in1=p_[:, 2:4, :], op=ALU.add), reads=[p_.r], writes=[p_.r])
                P.op('dve', lambda e, p_=p_, a_=a_, dc=dc: e.tensor_tensor(out=a_[:, dc, :], in0=p_[:, 0, :], in1=p_[:, 1, :], op=ALU.add),
                     reads=[p_.r], writes=[a_.r])
            for sub in range(2):
                py0, py1 = py[2 * sub], py[2 * sub + 1]
                for half, pyh in ((0, py0), (1, py1)):
                    for k in range(8):
                        P.op('pe', lambda e, k=k, half=half, pyh=pyh, a_=a_, sub=sub: e.matmul(
                            pyh[:], lhsT=a_[:, k, sub * 128:(sub + 1) * 128], rhs=wo[:, k, half * 512:(half + 1) * 512],
                            start=(k == 0), stop=(k == 7)), reads=[a_.r, wo.r], writes=[pyh.r])
                epi(tt_ * 2 + sub, py0, py1)
        P.barrier()
        P.emit()


def phase_ffn_up(C, l, hT):
    P, nc = C.P, C.nc
    mk = C.mk
    TOK = [(0, 256)] + [(256 + 512 * i, 512) for i in range(8)]
    with ExitStack() as es:
        wv = [mk(es, f"fw{i}", [128, 8, 256], BF16) for i in range(3)]
        dwf = mk(es, "fdwf", [128, 2 * NFC, 3], F32)
        dwb = mk(es, "fdwb", [128, 2 * NFC], F32)
        diag = [mk(es, f"fdiag{i}", [128, 6, 128], BF16) for i in range(2)]
        ub = [mk(es, f"fub{i}", [128, 2, T + 4], BF16) for i in range(2)]
        sa = [mk(es, f"fsa{i}", [128, 512], F32) for i in range(2)]
        mo = [mk(es, f"fmo{i}", [128, 512], BF16) for i in range(3)]
        pu = [mk(es, f"fpu{i}", [128, 512], F32, psum=True) for i in range(4)]
        pc = [mk(es, f"fpc{i}", [128, 512], F32, psum=True) for i in range(4)]

        def uoff(t0):
            return t0 + 1 if t0 < 256 else t0 + 3

        P.dma('sp', [lambda e: e.dma_start(out=dwf[:], in_=C.f_dw[l]), lambda e: e.dma_start(out=dwb[:], in_=C.f_dw_b[l])],
              writes=[dwf.r, dwb.r])
        for i in range(2):
            P.op('dve', lambda e, i=i: e.memset(ub[i][:], 0.0), writes=[ub[i].r])
        it = 0
        im = 0
        for c in range(NFC):
            w_, dg_, u_ = wv[c % 3], diag[c % 2], ub[c % 2]
            load_w(C, w_, C.f_up[l], c * 256, 256)
            for half in range(2):
                for j in range(3):
                    P.op('dve', lambda e, half=half, j=j, c=c, dg_=dg_: e.tensor_scalar(
                        out=dg_[:, half * 3 + j, :], in0=C.identf[:], scalar1=dwf[:, 2 * c + half, j:j + 1], scalar2=None, op0=ALU.mult),
                        reads=[C.identf.r, dwf.r], writes=[dg_.r])
            for (t0, n) in TOK:
                for half in range(2):
                    p_ = pu[it % 4]
                    it += 1
                    for k in range(8):
                        P.op('pe', lambda e, k=k, p_=p_, half=half, w_=w_, t0=t0, n=n: e.matmul(
                            p_[:, 0:n], lhsT=w_[:, k, half * 128:(half + 1) * 128], rhs=hT[:, k, t0:t0 + n],
                            start=(k == 0), stop=(k == 7)), reads=[w_.r, hT.r], writes=[p_.r])
                    if half == 0:
                        P.op('act', lambda e, p_=p_, u_=u_, t0=t0, n=n: e.activation(out=u_[:, 0, uoff(t0):uoff(t0) + n], in_=p_[:, 0:n], func=AF.Copy),
                             reads=[p_.r], writes=[u_.r])
                    else:
                        P.op('dve', lambda e, p_=p_, u_=u_, t0=t0, n=n: e.tensor_copy(out=u_[:, 1, uoff(t0):uoff(t0) + n], in_=p_[:, 0:n]),
                             reads=[p_.r], writes=[u_.r])
            for (t0, n) in TOK:
                uo = uoff(t0)
                pa_, pb_ = pc[(im % 2) * 2], pc[(im % 2) * 2 + 1]
                s_, m_ = sa[im % 2], mo[im % 3]
                im += 1
                for half, pd in ((0, pa_), (1, pb_)):
                    for j in range(3):
                        P.op('pe', lambda e, half=half, j=j, pd=pd, dg_=dg_, u_=u_, uo=uo, n=n: e.matmul(
                            pd[:, 0:n], lhsT=dg_[:, half * 3 + j, :], rhs=u_[:, half, uo + j - 1:uo + j - 1 + n],
                            start=(j == 0), stop=(j == 2)), reads=[dg_.r, u_.r], writes=[pd.r])
                P.op('act', lambda e, pa_=pa_, s_=s_, c=c, n=n: e.activation(out=s_[:, 0:n], in_=pa_[:, 0:n], func=AF.Silu,
                                                                            bias=dwb[:, 2 * c:2 * c + 1], scale=1.0),
                     reads=[pa_.r, dwb.r], writes=[s_.r])
                P.op('dve', lambda e, pb_=pb_, s_=s_, m_=m_, c=c, n=n: e.scalar_tensor_tensor(
                    out=m_[:, 0:n], in0=pb_[:, 0:n], scalar=dwb[:, 2 * c + 1:2 * c + 2], in1=s_[:, 0:n], op0=ALU.add, op1=ALU.mult),
                    reads=[pb_.r, dwb.r, s_.r], writes=[m_.r])
                P.dma('sp', lambda e, m_=m_, c=c, t0=t0, n=n: e.dma_start(out=C.MT[:, c, t0:t0 + n], in_=m_[:, 0:n]),
                      reads=[m_.r], writes=[C.rMT])
        P.barrier()
        P.emit()


def phase_ffn_down(C, l, wd):
    P, nc = C.P, C.nc
    mk = C.mk
    with ExitStack() as es:
        mt = [mk(es, f"dmt{i}", [128, NFC, 256], BF16) for i in range(2)]
        py = [mk(es, f"dpy{i}", [128, 512], F32, psum=True) for i in range(4)]
        epi = resid_epilogue(C, es, "de", l, 1)
        n = 0
        def dload(tt_):
            m_ = mt[tt_ % 2]
            P.dma('sp', lambda e, m_=m_, t0=tt_ * 256: e.dma_start(out=m_[:], in_=C.MT[:, :, t0:t0 + 256]), reads=[C.rMT], writes=[m_.r])

        dload(0)
        for tt_ in range(T // 256):
            t0 = tt_ * 256
            m_ = mt[tt_ % 2]
            if tt_ + 1 < T // 256:
                dload(tt_ + 1)
            for sub in range(2):
                py0, py1 = py[(n % 2) * 2], py[(n % 2) * 2 + 1]
                n += 1
                for half, pyh in ((0, py0), (1, py1)):
                    for k in range(NFC):
                        P.op('pe', lambda e, k=k, half=half, pyh=pyh, m_=m_, sub=sub: e.matmul(
                            pyh[:], lhsT=m_[:, k, sub * 128:(sub + 1) * 128], rhs=wd[:, k, half * 512:(half + 1) * 512],
                            start=(k == 0), stop=(k == NFC - 1)), reads=[m_.r, wd.r], writes=[pyh.r])
                epi(tt_ * 2 + sub, py0, py1)
        P.barrier()
        P.emit()


def _rope_tables():
    half = 32
    inv = (10000.0 ** (-(np.arange(0, half, 2, dtype=np.float32)) / half)).astype(np.float32)
    tpos = np.arange(4096)
    row = (tpos // 64).astype(np.float32)
    col = (tpos % 64).astype(np.float32)
    ang = np.concatenate([row[:, None] * inv, col[:, None] * inv], axis=-1).astype(np.float32)
    cos = np.concatenate([np.ones((256, 32), np.float32), np.cos(ang).astype(np.float32)], 0)
    sin = np.concatenate([np.zeros((256, 32), np.float32), np.sin(ang).astype(np.float32)], 0)
    tab = np.stack([cos, sin], 0).reshape(2, NT, 128, 32).transpose(0, 2, 1, 3)
    return np.ascontiguousarray(tab)


def _const_tables():
    p = np.arange(128, dtype=np.float32)
    cols = np.stack([p + 1, 128 - p, 127 - p, p], 1)
    j = p[:, None]
    i = p[None, :]
    dpos = np.maximum(i - j, 0)
    dneg = np.maximum(j - i, 0)
    mpos = (i >= j).astype(np.float32)
    mneg = (j >= i).astype(np.float32)
    ret = np.concatenate([cols, dpos, dneg, mpos, mneg], 1).astype(np.float32)
    mlo = (i <= j).astype(np.float32)
    mhi = (j <= i).astype(np.float32)
    mask = np.concatenate([np.tile(mlo, (1, 4)), np.tile(mhi, (1, 4))], 1).astype(np.float32)
    return ret, mask


def _prep_shared(inp):
    f = lambda a: np.ascontiguousarray(np.asarray(a, dtype=np.float32))
    w_in = np.asarray(inp["w_in"], dtype=np.float32)
    perm = []
    for c in range(4):
        perm += list(range(c * 128, (c + 1) * 128)) + list(range(512 + c * 128, 512 + (c + 1) * 128))
    hp = [0, 4, 1, 5, 2, 6, 3, 7]
    o = 1024
    for h in hp:
        perm += list(range(o + h * 64, o + (h + 1) * 64))
    perm += list(range(o + 512, o + 768))
    o = 1024 + 768
    perm += list(range(o, o + 1536))
    o = 1024 + 768 + 1536
    for h in hp:
        perm += list(range(o + h * 64, o + (h + 1) * 64))
    perm += list(range(o + 512, o + 768))
    perm += list(range(4096, 8192))
    perm = np.asarray(perm)
    assert perm.shape[0] == 8192 and np.unique(perm).shape[0] == 8192
    fperm = []
    for c in range(NFC):
        fperm += list(range(c * 128, (c + 1) * 128)) + list(range(DFF + c * 128, DFF + (c + 1) * 128))
    fperm = np.asarray(fperm)
    ret, mask = _const_tables()
    a_vec = np.stack([np.asarray(inp[k], np.float32).reshape(DEPTH, 4, 128) for k in ("a_dw_b", "a_ln_g", "a_ln_b")], 1)
    sh = {
        "ada_w": f(inp["ada_w"]), "ada_b": f(inp["ada_b"]), "norm_g": f(inp["norm_g"]),
        "w_in": f(w_in[:, :, perm]),
        "a_dw": f(np.asarray(inp["a_dw"], np.float32).reshape(DEPTH, 31, 4, 128).transpose(0, 3, 2, 1)),
        "a_vec": f(a_vec.transpose(0, 3, 1, 2)),
        "b_sink": f(inp["b_sink"]), "c_decay": f(np.asarray(inp["c_decay_logit"], np.float32).reshape(DEPTH, 8)),
        "c_gn_g": f(inp["c_gn_g"]),
        "d_qk_g": f(np.stack([np.asarray(inp["d_qn_g"], np.float32), np.asarray(inp["d_kn_g"], np.float32)], 1)),
        "w_br": f(inp["w_br"]), "w_o": f(inp["w_o"]),
        "f_up": f(np.asarray(inp["f_up"], np.float32)[:, :, fperm]),
        "f_dw": f(np.asarray(inp["f_dw"], np.float32)[:, :, fperm].reshape(DEPTH, 3, 2 * NFC, 128).transpose(0, 3, 2, 1)),
        "f_dw_b": f(np.asarray(inp["f_dw_b"], np.float32)[:, fperm].reshape(DEPTH, 2 * NFC, 128).transpose(0, 2, 1)),
        "f_down": f(inp["f_down"]),
        "cst_rope": _rope_tables(), "cst_ret": ret, "cst_mask": mask, "cst_ident": np.eye(128, dtype=np.float32),
    }
    return sh


def _prep_core(inp, b):
    x = np.asarray(inp["x"], np.float32)
    ctx = np.asarray(inp["ctx"], np.float32)
    c = np.asarray(inp["c"], np.float32)
    c_ctx = np.asarray(inp["c_ctx"], np.float32)
    xin = np.ascontiguousarray(np.concatenate([ctx[b], x[b]], 0))
    ccv = np.stack([c[b], c_ctx], 1).reshape(8, 128, 2).transpose(1, 0, 2)
    return {"xin": xin, "cc": np.ascontiguousarray(ccv)}


def kernel(**inputs):
    nc = build_program()
    sh = _prep_shared(inputs)
    in_maps = []
    for b in range(8):
        m = dict(sh)
        m.update(_prep_core(inputs, b))
        in_maps.append(m)
    res = run_bass_kernel_spmd(nc, in_maps, core_ids=list(range(8)))
    return np.stack([np.asarray(r["y"], dtype=np.float32) for r in res.results], 0)
```
